# Optimizing a Trainium2 kernel written in Bass

```python
import jax, jax.numpy as jnp
from jax import lax
import numpy as np

D_MODEL = 1024
BATCH = 8
SEQ = 2048
DEPTH = 4
DEC_BATCH = 128
DEC_SEQ = 4
PAST_LEN = 16384
PAGE_SIZE = 128

N_META = 16
N_PAIRS = DEPTH // 2
GLA_HEADS = 4
GLA_DK = 64
GLA_DV = 128
GLA_KEY = GLA_HEADS * GLA_DK
GLA_VAL = GLA_HEADS * GLA_DV
GLA_RANK = 16
GLA_GATE_NORM = 16.0
GLA_CHUNK = 16
LRU_WIDTH = 512
LRU_BLOCKS = 8
LRU_BDIM = LRU_WIDTH // LRU_BLOCKS
CONV_W = 4
LRU_C = 8.0
MIX_IN = 2 * GLA_KEY + 2 * GLA_VAL + GLA_RANK + 2 * LRU_WIDTH
MIX_OUT = GLA_VAL + LRU_WIDTH
RWKV_HEAD = 64
RWKV_HEADS = D_MODEL // RWKV_HEAD
DECAY_LORA = 64
AAA_LORA = 64
MV_LORA = 32
GATE_LORA = 128
RWKV_GN_EPS = 64e-5
PEER_HEADS = 8
PEER_NKEYS = 128
PEER_EXPERTS = PEER_NKEYS * PEER_NKEYS
PEER_DKEY = 256
PEER_HALF = PEER_DKEY // 2
PEER_TOPK = 16
PEER_BLOCK = 128
DN_ALPHA = float((2 * DEPTH) ** 0.25)
DN_BETA = float((8 * DEPTH) ** -0.25)
LN_EPS = 1e-5
F32 = jnp.float32

kernel_name = 'hybrid_gla_rglru_rwkv7_peer_step'


def layer_norm(x, g, b):
    xf = x.astype(F32)
    mu = xf.mean(-1, keepdims=True)
    var = jnp.mean(jnp.square(xf - mu), -1, keepdims=True)
    return ((xf - mu) * lax.rsqrt(var + LN_EPS) * g + b).astype(x.dtype)


def split_cols(z, sizes):
    out, o = [], 0
    for s in sizes:
        out.append(z[..., o:o + s])
        o += s
    return out


def gla_chunked(q, k, v, log_a, s0):
    B, T = q.shape[:2]
    n = -(-T // GLA_CHUNK)
    pad = n * GLA_CHUNK - T
    padf = lambda z: jnp.pad(z.astype(F32), ((0, 0), (0, pad), (0, 0), (0, 0)))
    blk = lambda z: z.reshape(B, n, GLA_CHUNK, z.shape[2], z.shape[3]).transpose(1, 0, 3, 2, 4)
    q, k, v, log_a = (blk(padf(z)) for z in (q, k, v, log_a))
    cum = jnp.cumsum(log_a, axis=3)
    last = cum[:, :, :, -1:, :]
    qe = q * jnp.exp(cum)
    ke = k * jnp.exp(-cum)
    kl = k * jnp.exp(last - cum)
    mask = jnp.tril(jnp.ones((GLA_CHUNK, GLA_CHUNK), bool))
    att = jnp.where(mask, jnp.einsum('nbhtd,nbhsd->nbhts', qe, ke), 0.0)
    o_intra = jnp.einsum('nbhts,nbhsv->nbhtv', att, v)
    upd = jnp.einsum('nbhsd,nbhsv->nbhdv', kl, v)
    decay = jnp.exp(last[:, :, :, 0, :])

    def step(S, inp):
        dec, u = inp
        return S * dec[..., None] + u, S

    s_fin, s_prev = lax.scan(step, s0.astype(F32), (decay, upd))
    o_inter = jnp.einsum('nbhtd,nbhdv->nbhtv', qe, s_prev)
    o = (o_intra + o_inter).transpose(1, 0, 3, 2, 4).reshape(B, n * GLA_CHUNK, GLA_HEADS, GLA_DV)[:, :T]
    return o, s_fin


def lru_scan(a, b, h0):
    b = b.at[:, 0].add(a[:, 0] * h0)

    def comb(l, r):
        al, bl = l
        ar, br = r
        return al * ar, ar * bl + br

    _, h = lax.associative_scan(comb, (a, b), axis=1)
    return h, h[:, -1]


def gla_lru_mixer(x, s_gla, h_lru, conv_buf, w_in, w_gate, b_gate, gla_norm, conv_w, conv_b,
                  lru_wa, lru_ba, lru_wx, lru_bx, lru_lambda, w_out):
    B, T, _ = x.shape
    proj = x @ w_in
    q, k, v, g_lr, r, xb, gb = split_cols(proj, [GLA_KEY, GLA_KEY, GLA_VAL, GLA_RANK, GLA_VAL, LRU_WIDTH, LRU_WIDTH])
    log_a = jax.nn.log_sigmoid((g_lr @ w_gate + b_gate).astype(F32)) / GLA_GATE_NORM
    hd = lambda z, d: z.reshape(B, T, GLA_HEADS, d)
    o, s_gla_new = gla_chunked(hd(q, GLA_DK) * GLA_DK ** -0.5, hd(k, GLA_DK), hd(v, GLA_DV), hd(log_a, GLA_DK), s_gla)
    o = o * lax.rsqrt(jnp.mean(jnp.square(o), -1, keepdims=True) + LN_EPS) * gla_norm
    o_gla = o.reshape(B, T, GLA_VAL) * jax.nn.silu(r.astype(F32))
    xcat = jnp.concatenate([conv_buf.astype(xb.dtype), xb], axis=1)
    xc = conv_b + sum(xcat[:, i:i + T] * conv_w[i] for i in range(CONV_W))
    new_buf = xcat[:, T:]
    xr = xc.reshape(B, T, LRU_BLOCKS, LRU_BDIM)
    gate_a = jax.nn.sigmoid((jnp.einsum('btnd,nde->btne', xr, lru_wa).reshape(B, T, LRU_WIDTH) + lru_ba).astype(F32))
    gate_x = jax.nn.sigmoid((jnp.einsum('btnd,nde->btne', xr, lru_wx).reshape(B, T, LRU_WIDTH) + lru_bx).astype(F32))
    log_at = -LRU_C * gate_a * jax.nn.softplus(-lru_lambda.astype(F32))
    a_t = jnp.exp(log_at)
    b_t = jnp.sqrt(-jnp.expm1(2.0 * log_at)) * (gate_x * xc.astype(F32))
    h, h_last = lru_scan(a_t, b_t, h_lru.astype(F32))
    y_lru = h * jax.nn.gelu(gb.astype(F32))
    out = jnp.concatenate([o_gla, y_lru], axis=-1).astype(x.dtype) @ w_out
    return out, s_gla_new.astype(x.dtype), h_last.astype(x.dtype), new_buf.astype(x.dtype)


def wkv7_scan(r, w, k, v, a, b, s0):
    tm = lambda z: jnp.moveaxis(z, 1, 0)

    def step(S, inp):
        r_t, w_t, k_t, v_t, a_t, b_t = inp
        sa = jnp.einsum('bhvk,bhk->bhv', S, a_t)
        S = S * w_t[:, :, None, :] + sa[..., None] * b_t[:, :, None, :] + v_t[..., None] * k_t[:, :, None, :]
        return S, jnp.einsum('bhvk,bhk->bhv', S, r_t)

    s_fin, y = lax.scan(step, s0.astype(F32), (tm(r), tm(w), tm(k), tm(v), tm(a), tm(b)))
    return jnp.moveaxis(y, 0, 1), s_fin


def rwkv7_mixer(x, s0, shift, v_first, vres, mu, w_r, w_k, w_v, w_o, w0, w1, w2, a0, a1, a2,
                g1, g2, k_k, k_a, r_k, lnx_g, lnx_b):
    B, T, D = x.shape
    x_prev = jnp.concatenate([shift[:, None, :].astype(x.dtype), x[:, :-1]], axis=1)
    xx = x_prev - x
    xr, xw, xk, xv, xa, xg = (x + xx * mu[i] for i in range(6))
    r = xr @ w_r
    w = -jax.nn.softplus(-(w0 + jnp.tanh(xw @ w1) @ w2).astype(F32)) - 0.5
    k = xk @ w_k
    v = xv @ w_v
    if vres is None:
        v_first = v
    else:
        v0, v1, v2 = vres
        v = v + (v_first - v) * jax.nn.sigmoid(v0 + (xv @ v1) @ v2)
    a = jax.nn.sigmoid(a0 + (xa @ a1) @ a2)
    g = jax.nn.sigmoid(xg @ g1) @ g2
    hd = lambda z: z.reshape(B, T, RWKV_HEADS, RWKV_HEAD).astype(F32)
    kk = hd(k * k_k)
    kk = kk / jnp.maximum(jnp.sqrt(jnp.sum(jnp.square(kk), -1, keepdims=True)), 1e-12)
    k = k * (1 + (a - 1) * k_a)
    rh, kh, vh, ah = hd(r), hd(k), hd(v), hd(a)
    decay = jnp.exp(-jnp.exp(hd(w)))
    y, s_new = wkv7_scan(rh, decay, kh, vh, -kk, kk * ah, s0)
    mu_y = y.mean(-1, keepdims=True)
    var_y = jnp.mean(jnp.square(y - mu_y), -1, keepdims=True)
    y = ((y - mu_y) * lax.rsqrt(var_y + RWKV_GN_EPS)).reshape(B, T, D) * lnx_g + lnx_b
    bonus = jnp.sum(rh * kh * r_k, -1, keepdims=True) * vh
    y = y + bonus.reshape(B, T, D)
    out = (y * g).astype(x.dtype) @ w_o
    return out, s_new.astype(x.dtype), x[:, -1], v_first


def peer_ffn(x, w_q, keys, u_tab, v_tab):
    B, T, D = x.shape
    n = B * T
    nb = -(-n // PEER_BLOCK)
    xf = jnp.pad(x.reshape(n, D), ((0, nb * PEER_BLOCK - n), (0, 0))).reshape(nb, PEER_BLOCK, D)
    kf = keys.astype(F32)

    def block(xb):
        q = (xb @ w_q).reshape(PEER_BLOCK, PEER_HEADS, PEER_DKEY).astype(F32)
        qm = q.mean(-1, keepdims=True)
        q = (q - qm) * lax.rsqrt(jnp.mean(jnp.square(q - qm), -1, keepdims=True) + LN_EPS)
        s1 = jnp.einsum('thd,hkd->thk', q[..., :PEER_HALF], kf[:, 0])
        s2 = jnp.einsum('thd,hkd->thk', q[..., PEER_HALF:], kf[:, 1])
        v1, i1 = lax.top_k(s1, PEER_TOPK)
        v2, i2 = lax.top_k(s2, PEER_TOPK)
        cand = (v1[..., :, None] + v2[..., None, :]).reshape(PEER_BLOCK, PEER_HEADS, PEER_TOPK * PEER_TOPK)
        cidx = (i1[..., :, None] * PEER_NKEYS + i2[..., None, :]).reshape(PEER_BLOCK, PEER_HEADS, PEER_TOPK * PEER_TOPK)
        sc, pos = lax.top_k(cand, PEER_TOPK)
        eidx = jnp.take_along_axis(cidx, pos, axis=-1)
        gate = jax.nn.softmax(sc, axis=-1)
        act = jax.nn.gelu(jnp.einsum('thkd,td->thk', u_tab[eidx], xb).astype(F32))
        coef = (gate * act).astype(xb.dtype)
        return jnp.einsum('thk,thkd->td', coef, v_tab[eidx])

    y = lax.map(block, xf)
    return y.reshape(nb * PEER_BLOCK, D)[:n].reshape(B, T, D)


def trunk(x, st_gla, st_h, st_conv, st_rwkv, st_shift, W):
    new_gla, new_h, new_conv, new_rwkv, new_shift = [], [], [], [], []
    v_first = None
    for layer in range(DEPTH):
        j = layer // 2
        if layer % 2 == 0:
            m, s1, s2, s3 = gla_lru_mixer(
                x, st_gla[j], st_h[j], st_conv[j], W['ev_w_in'][j], W['ev_gla_w_gate'][j], W['ev_gla_b_gate'][j],
                W['ev_gla_norm'][j], W['ev_conv_w'][j], W['ev_conv_b'][j], W['ev_lru_wa'][j], W['ev_lru_ba'][j],
                W['ev_lru_wx'][j], W['ev_lru_bx'][j], W['ev_lru_lambda'][j], W['ev_w_out'][j])
            new_gla.append(s1)
            new_h.append(s2)
            new_conv.append(s3)
        else:
            vres = None if j == 0 else (W['od_v0'][j - 1], W['od_v1'][j - 1], W['od_v2'][j - 1])
            m, s4, s5, v_first = rwkv7_mixer(
                x, st_rwkv[j], st_shift[j], v_first, vres, W['od_mu'][j], W['od_w_r'][j], W['od_w_k'][j],
                W['od_w_v'][j], W['od_w_o'][j], W['od_w0'][j], W['od_w1'][j], W['od_w2'][j], W['od_a0'][j],
                W['od_a1'][j], W['od_a2'][j], W['od_g1'][j], W['od_g2'][j], W['od_k_k'][j], W['od_k_a'][j],
                W['od_r_k'][j], W['od_lnx_g'][j], W['od_lnx_b'][j])
            new_rwkv.append(s4)
            new_shift.append(s5)
        x = layer_norm(DN_ALPHA * x + m, W['ln_g'][layer, 0], W['ln_b'][layer, 0])
        f = peer_ffn(x, W['peer_w_q'][layer], W['peer_keys'][layer], W['peer_u'][layer], W['peer_v'][layer])
        x = layer_norm(DN_ALPHA * x + f, W['ln_g'][layer, 1], W['ln_b'][layer, 1])
    return x, jnp.stack(new_gla), jnp.stack(new_h), jnp.stack(new_conv), jnp.stack(new_rwkv), jnp.stack(new_shift)


def setup_inputs(seed: int = 0) -> dict:
    key = jax.random.key(seed)
    ks = iter(jax.random.split(key, 64))
    nrm = lambda shape, scale: jax.random.normal(next(ks), shape, F32) * scale
    D = D_MODEL
    NP = N_PAIRS
    a_init = jax.random.uniform(next(ks), (NP, LRU_WIDTH), F32, minval=0.9, maxval=0.999)
    s_init = a_init ** (1.0 / LRU_C)
    lam = jnp.log(s_init) - jnp.log1p(-s_init)
    ramp = (jnp.arange(D, dtype=F32) / (D - 1)) ** 0.9
    return {
        'x_prompt': nrm((BATCH, SEQ, D), 1.0),
        'x_sample': nrm((DEC_BATCH, DEC_SEQ, D), 1.0),
        'state_gla': nrm((NP, DEC_BATCH, GLA_HEADS, GLA_DK, GLA_DV), 0.1),
        'state_lru_h': nrm((NP, DEC_BATCH, LRU_WIDTH), 0.5),
        'state_lru_conv': nrm((NP, DEC_BATCH, CONV_W - 1, LRU_WIDTH), 1.0),
        'state_rwkv': nrm((NP, DEC_BATCH, RWKV_HEADS, RWKV_HEAD, RWKV_HEAD), 0.1),
        'state_rwkv_shift': nrm((NP, DEC_BATCH, D), 1.0),
        'meta_tokens': nrm((N_META, D), 1.0),
        'ln_g': 1.0 + nrm((DEPTH, 2, D), 0.01),
        'ln_b': nrm((DEPTH, 2, D), 0.01),
        'ev_w_in': nrm((NP, D, MIX_IN), D ** -0.5),
        'ev_gla_w_gate': nrm((NP, GLA_RANK, GLA_KEY), GLA_RANK ** -0.5),
        'ev_gla_b_gate': nrm((NP, GLA_KEY), 0.1),
        'ev_gla_norm': 1.0 + nrm((NP, GLA_DV), 0.01),
        'ev_conv_w': nrm((NP, CONV_W, LRU_WIDTH), CONV_W ** -0.5),
        'ev_conv_b': nrm((NP, LRU_WIDTH), 0.01),
        'ev_lru_wa': nrm((NP, LRU_BLOCKS, LRU_BDIM, LRU_BDIM), LRU_BDIM ** -0.5),
        'ev_lru_ba': nrm((NP, LRU_WIDTH), 0.1),
        'ev_lru_wx': nrm((NP, LRU_BLOCKS, LRU_BDIM, LRU_BDIM), LRU_BDIM ** -0.5),
        'ev_lru_bx': nrm((NP, LRU_WIDTH), 0.1),
        'ev_lru_lambda': lam,
        'ev_w_out': nrm((NP, MIX_OUT, D), DN_BETA * MIX_OUT ** -0.5),
        'od_mu': jax.random.uniform(next(ks), (NP, 6, D), F32),
        'od_w_r': nrm((NP, D, D), D ** -0.5),
        'od_w_k': nrm((NP, D, D), D ** -0.5),
        'od_w_v': nrm((NP, D, D), DN_BETA * D ** -0.5),
        'od_w_o': nrm((NP, D, D), DN_BETA * D ** -0.5),
        'od_w0': -6.0 + 5.0 * ramp + nrm((NP, D), 0.1),
        'od_w1': nrm((NP, D, DECAY_LORA), D ** -0.5),
        'od_w2': nrm((NP, DECAY_LORA, D), 0.1 * DECAY_LORA ** -0.5),
        'od_a0': nrm((NP, D), 0.1),
        'od_a1': nrm((NP, D, AAA_LORA), D ** -0.5),
        'od_a2': nrm((NP, AAA_LORA, D), AAA_LORA ** -0.5),
        'od_v0': 1.0 + nrm((NP - 1, D), 0.1),
        'od_v1': nrm((NP - 1, D, MV_LORA), D ** -0.5),
        'od_v2': nrm((NP - 1, MV_LORA, D), MV_LORA ** -0.5),
        'od_g1': nrm((NP, D, GATE_LORA), D ** -0.5),
        'od_g2': nrm((NP, GATE_LORA, D), GATE_LORA ** -0.5),
        'od_k_k': 0.85 + nrm((NP, D), 0.02),
        'od_k_a': 1.0 + nrm((NP, D), 0.02),
        'od_r_k': nrm((NP, RWKV_HEADS, RWKV_HEAD), 0.1),
        'od_lnx_g': 1.0 + nrm((NP, D), 0.01),
        'od_lnx_b': nrm((NP, D), 0.01),
        'peer_w_q': nrm((DEPTH, D, PEER_HEADS * PEER_DKEY), D ** -0.5),
        'peer_keys': nrm((DEPTH, PEER_HEADS, 2, PEER_NKEYS, PEER_HALF), PEER_HALF ** -0.5),
        'peer_u': nrm((DEPTH, PEER_EXPERTS, D), D ** -0.5),
        'peer_v': nrm((DEPTH, PEER_EXPERTS, D), DN_BETA * PEER_HEADS ** -0.5),
    }


def reference(x_prompt, x_sample, state_gla, state_lru_h, state_lru_conv, state_rwkv, state_rwkv_shift,
              meta_tokens, ln_g, ln_b, ev_w_in, ev_gla_w_gate, ev_gla_b_gate, ev_gla_norm, ev_conv_w, ev_conv_b,
              ev_lru_wa, ev_lru_ba, ev_lru_wx, ev_lru_bx, ev_lru_lambda, ev_w_out, od_mu, od_w_r, od_w_k, od_w_v,
              od_w_o, od_w0, od_w1, od_w2, od_a0, od_a1, od_a2, od_v0, od_v1, od_v2, od_g1, od_g2, od_k_k, od_k_a,
              od_r_k, od_lnx_g, od_lnx_b, peer_w_q, peer_keys, peer_u, peer_v):
    W = dict(ln_g=ln_g, ln_b=ln_b, ev_w_in=ev_w_in, ev_gla_w_gate=ev_gla_w_gate, ev_gla_b_gate=ev_gla_b_gate,
             ev_gla_norm=ev_gla_norm, ev_conv_w=ev_conv_w, ev_conv_b=ev_conv_b, ev_lru_wa=ev_lru_wa,
             ev_lru_ba=ev_lru_ba, ev_lru_wx=ev_lru_wx, ev_lru_bx=ev_lru_bx, ev_lru_lambda=ev_lru_lambda,
             ev_w_out=ev_w_out, od_mu=od_mu, od_w_r=od_w_r, od_w_k=od_w_k, od_w_v=od_w_v, od_w_o=od_w_o,
             od_w0=od_w0, od_w1=od_w1, od_w2=od_w2, od_a0=od_a0, od_a1=od_a1, od_a2=od_a2, od_v0=od_v0,
             od_v1=od_v1, od_v2=od_v2, od_g1=od_g1, od_g2=od_g2, od_k_k=od_k_k, od_k_a=od_k_a, od_r_k=od_r_k,
             od_lnx_g=od_lnx_g, od_lnx_b=od_lnx_b, peer_w_q=peer_w_q, peer_keys=peer_keys, peer_u=peer_u,
             peer_v=peer_v)
    Bp = x_prompt.shape[0]
    dt = x_prompt.dtype
    xp = jnp.concatenate([jnp.broadcast_to(meta_tokens.astype(dt)[None], (Bp, N_META, D_MODEL)), x_prompt], axis=1)
    z_gla = jnp.zeros((N_PAIRS, Bp, GLA_HEADS, GLA_DK, GLA_DV), dt)
    z_h = jnp.zeros((N_PAIRS, Bp, LRU_WIDTH), dt)
    z_conv = jnp.zeros((N_PAIRS, Bp, CONV_W - 1, LRU_WIDTH), dt)
    z_rwkv = jnp.zeros((N_PAIRS, Bp, RWKV_HEADS, RWKV_HEAD, RWKV_HEAD), dt)
    z_shift = jnp.zeros((N_PAIRS, Bp, D_MODEL), dt)
    yp, p_gla, p_h, p_conv, p_rwkv, p_shift = trunk(xp, z_gla, z_h, z_conv, z_rwkv, z_shift, W)
    y_prompt = yp[:, N_META:]
    y_sample, s_gla, s_h, s_conv, s_rwkv, s_shift = trunk(
        x_sample, state_gla, state_lru_h, state_lru_conv, state_rwkv, state_rwkv_shift, W)
    return (y_prompt, y_sample, p_gla, p_h, p_conv, p_rwkv, p_shift, s_gla, s_h, s_conv, s_rwkv, s_shift)
```

```python
import numpy as np
import concourse.bass as bass
import concourse.mybir as mybir
from concourse.bass_utils import run_bass_kernel_spmd

F32 = mybir.dt.float32
BF16 = mybir.dt.bfloat16
U32 = mybir.dt.uint32
AF = mybir.ActivationFunctionType
ALU = mybir.AluOpType
AX = mybir.AxisListType

NCORES = 8
D = 1024
DEPTH = 4
NMETA = 16
TP = 2064
NSS = 16
TS = 4
DN_ALPHA = float((2 * DEPTH) ** 0.25)
LN_EPS = 1e-5
MIX_IN = 2576


class Tl:
    def __init__(self, ap):
        self.ap = ap
        self.buf = self
        self.bufs = [self]
        self.w = None
        self.r = []

    def __getitem__(self, k):
        return Vw(self.ap[k], self)

    def v(self, ap):
        return Vw(ap, self)

    def bitcast(self, dt):
        return Vw(self.ap.bitcast(dt), self)

    def re(self, pat, **kw):
        return Vw(self.ap.rearrange(pat, **kw), self)

    def bc(self, shape):
        return Vw(self.ap.to_broadcast(list(shape)), self)


class Vw:
    def __init__(self, ap, buf):
        self.ap = ap
        self.buf = buf
        self.bufs = buf if isinstance(buf, list) else [buf]

    def __getitem__(self, k):
        return Vw(self.ap[k], self.bufs)

    def bc(self, shape):
        return Vw(self.ap.to_broadcast(list(shape)), self.bufs)

    def re(self, pat, **kw):
        return Vw(self.ap.rearrange(pat, **kw), self.bufs)

    def bitcast(self, dt):
        return Vw(self.ap.bitcast(dt), self.bufs)


def _ap(x):
    return x.ap if isinstance(x, (Tl, Vw)) else x


class Eng:
    def __init__(self, nc, name, h):
        self.name = name
        self.h = h
        self.sem = nc.alloc_semaphore("s_" + name)
        self.cnt = 0
        self.known = {}
        self.dsems = []
        self.dvals = []
        self.dma_i = 0


class FW:
    NDS = 8

    def __init__(self, nc):
        self.nc = nc
        self.pe = Eng(nc, "pe", nc.tensor)
        self.act = Eng(nc, "act", nc.scalar)
        self.dve = Eng(nc, "dve", nc.vector)
        self.pool = Eng(nc, "pool", nc.gpsimd)
        self.sp = Eng(nc, "sp", nc.sync)
        for q in (self.sp, self.pool, self.act):
            q.dsems = [nc.alloc_semaphore("d_%s%d" % (q.name, i)) for i in range(self.NDS)]
            q.dvals = [0] * self.NDS
        self.out_events = []
        self.ninst = 0

    def sb(self, name, shape, dt=F32):
        return Tl(self.nc.alloc_sbuf_tensor(name, list(shape), dt).ap())

    def arena_init(self, nbytes, reg=8192):
        self.a_reg = reg
        self.a_big = self.nc.alloc_sbuf_tensor("arena", [128, nbytes // 2], BF16).ap()
        self.a_tl = [Tl(self.a_big[:, k * reg // 2:(k + 1) * reg // 2]) for k in range(nbytes // reg)]

    def av(self, off, shape, dt=F32):
        esz = 4 if dt in (F32, U32) else 2
        nel = int(np.prod(shape[1:]))
        nb = nel * esz
        assert off % 4 == 0 and off + nb <= len(self.a_tl) * self.a_reg, (off, nb)
        ap = self.a_big[:, off // 2:(off + nb) // 2]
        if esz == 4:
            ap = ap.bitcast(dt)
        if len(shape) > 2:
            names = " ".join("d%d" % i for i in range(len(shape) - 1))
            ap = ap.rearrange("p (%s) -> p %s" % (names, names), **{"d%d" % i: shape[i + 1] for i in range(len(shape) - 1)})
        bufs = self.a_tl[off // self.a_reg:(off + nb - 1) // self.a_reg + 1]
        return Vw(ap, list(bufs))

    def _wait(self, eng, ev):
        key, sem, val = ev
        if eng.known.get(key, 0) >= val:
            return
        eng.h.wait_ge(sem, val)
        self.ninst += 1
        eng.known[key] = val

    def _deps(self, eng, reads, writes):
        deps = []
        for v in reads:
            for b in v.bufs:
                if b.w is not None:
                    deps.append(b.w)
        for v in writes:
            for b in v.bufs:
                if b.w is not None:
                    deps.append(b.w)
                deps.extend(b.r)
        for ev in deps:
            if eng is self.pe and ev[0] == "pe":
                continue
            self._wait(eng, ev)

    def _record(self, ev, reads, writes):
        for v in reads:
            for b in v.bufs:
                b.r.append(ev)
        for v in writes:
            for b in v.bufs:
                b.w = ev
                b.r = []

    def op(self, eng, fn, reads=(), writes=()):
        reads = [b for b in reads if isinstance(b, (Tl, Vw))]
        writes = [b for b in writes if isinstance(b, (Tl, Vw))]
        self._deps(eng, reads, writes)
        ins = fn()
        eng.cnt += 1
        ins.then_inc(eng.sem, 1)
        self.ninst += 1
        self._record((eng.name, eng.sem, eng.cnt), reads, writes)

    def dma(self, q, out, in_, is_output=False, nc_ok=False):
        slot = q.dma_i % self.NDS
        q.dma_i += 1
        sem = q.dsems[slot]
        prev = q.dvals[slot]
        key = (q.name, "d", slot)
        if prev > 0:
            self._wait(q, (key, sem, prev))
        reads = [in_] if isinstance(in_, (Tl, Vw)) else []
        writes = [out] if isinstance(out, (Tl, Vw)) else []
        self._deps(q, reads, writes)
        kw = {}
        if nc_ok:
            kw["allow_slow_non_contiguous"] = True
        q.h.dma_start(out=_ap(out), in_=_ap(in_), **kw).then_inc(sem, 16)
        self.ninst += 1
        q.dvals[slot] = prev + 16
        ev = (key, sem, prev + 16)
        self._record(ev, reads, writes)
        if is_output:
            self.out_events.append(ev)

    def finish(self):
        for ev in self.out_events:
            self._wait(self.sp, ev)
        for e in (self.pe, self.act, self.dve, self.pool):
            if e.cnt:
                self._wait(self.sp, (e.name, e.sem, e.cnt))

    def mm(self, out, lhsT, rhs, start=True, stop=True):
        self.op(self.pe, lambda: self.nc.tensor.matmul(_ap(out), lhsT=_ap(lhsT), rhs=_ap(rhs), start=start, stop=stop),
                reads=[lhsT, rhs], writes=[out])

    def tr(self, out, in_, ident):
        self.op(self.pe, lambda: self.nc.tensor.transpose(_ap(out), _ap(in_), _ap(ident)),
                reads=[in_, ident], writes=[out])

    def actv(self, out, in_, func, bias=None, scale=1.0):
        kw = {}
        rd = [in_]
        if bias is not None:
            kw["bias"] = _ap(bias)
            rd.append(bias)
        if isinstance(scale, (Tl, Vw)):
            rd.append(scale)
        kw["scale"] = _ap(scale)
        self.op(self.act, lambda: self.nc.scalar.activation(out=_ap(out), in_=_ap(in_), func=func, **kw),
                reads=rd, writes=[out])

    def tt(self, out, a, b, op, eng=None):
        eng = eng or self.dve
        self.op(eng, lambda: eng.h.tensor_tensor(out=_ap(out), in0=_ap(a), in1=_ap(b), op=op),
                reads=[a, b], writes=[out])

    def ts(self, out, a, s1, s2, op0, op1=ALU.bypass, eng=None):
        eng = eng or self.dve
        self.op(eng, lambda: eng.h.tensor_scalar(out=_ap(out), in0=_ap(a), scalar1=_ap(s1), scalar2=_ap(s2), op0=op0, op1=op1),
                reads=[a, s1, s2], writes=[out])

    def stt(self, out, a, s, b, op0, op1):
        self.op(self.dve, lambda: self.nc.vector.scalar_tensor_tensor(out=_ap(out), in0=_ap(a), scalar=_ap(s), in1=_ap(b), op0=op0, op1=op1),
                reads=[a, s, b], writes=[out])

    def cp(self, out, in_, eng=None):
        eng = eng or self.dve
        if eng is self.act:
            self.op(eng, lambda: self.nc.scalar.copy(out=_ap(out), in_=_ap(in_)), reads=[in_], writes=[out])
        else:
            self.op(eng, lambda: eng.h.tensor_copy(out=_ap(out), in_=_ap(in_)), reads=[in_], writes=[out])

    def memset(self, out, val, eng=None):
        eng = eng or self.dve
        self.op(eng, lambda: eng.h.memset(_ap(out), val), writes=[out])

    def scan(self, out, d0, d1, init, op0=ALU.mult, op1=ALU.add):
        self.op(self.dve, lambda: self.nc.vector.tensor_tensor_scan(out=_ap(out), data0=_ap(d0), data1=_ap(d1), initial=_ap(init), op0=op0, op1=op1),
                reads=[d0, d1, init], writes=[out])


class Prog:
    def __init__(self, debug=None):
        self.debug = debug
        nc = self.nc = bass.Bass("TRN2", target_bir_lowering=False)
        self.fw = FW(nc)
        self.ins = {}
        self.outs = {}
        self.psn = 0
        self.nrot = 6

    def din(self, name, shape, dt=F32):
        t = self.nc.dram_tensor(name, list(shape), dt, kind="ExternalInput").ap()
        self.ins[name] = t
        return t

    def dout(self, name, shape, dt=F32):
        t = self.nc.dram_tensor(name, list(shape), dt, kind="ExternalOutput").ap()
        self.outs[name] = t
        return t

    def psum(self):
        t = self.ps[self.psn % self.nrot]
        self.psn += 1
        return t

    def build(self):
        fw = self.fw
        nc = self.nc
        sb = fw.sb
        self.o = {}
        xT_p = self.din("xT_p", [D, TP])
        xT_s = self.din("xT_s", [D, NSS * TS])
        cst = self.din("consts", [128, 2816])
        self.d_ev_w_in = self.din("ev_w_in", [2, D, MIX_IN])
        self.d_ev_w_gate = self.din("ev_w_gate", [2, 16, 256])
        self.d_ev_vecs = self.din("ev_vecs", [2, 128, 40])
        self.d_ev_lruw = self.din("ev_lruw", [2, 2, 4, 128, 128])
        self.d_ev_w_out = self.din("ev_w_out", [2, D, D])
        self.d_st_gla = self.din("st_gla", [2, 128, NSS * 2 * 128])
        self.d_st_h = self.din("st_h", [2, 128, 4 * NSS])
        self.d_st_conv = self.din("st_conv", [2, 128, 4 * NSS * 3])
        self.d_od_w = {nm: self.din("od_" + nm, [2, D, D]) for nm in ("w_r", "w_k", "w_v", "w_o")}
        self.d_od_l1 = self.din("od_l1", [2, D, 288])
        self.d_od_w2 = self.din("od_w2", [2, 64, D])
        self.d_od_a2 = self.din("od_a2", [2, 64, D])
        self.d_od_v2 = self.din("od_v2", [2, 32, D])
        self.d_od_g2 = self.din("od_g2", [2, 128, D])
        self.d_od_vecs = self.din("od_vecs", [2, 128, 112])
        self.d_st_rw = self.din("st_rw", [2, 8, 128, NSS * 64])
        self.d_st_sh = self.din("st_sh", [2, 128, 8 * NSS])
        self.o["rw_p"] = self.dout("o_rw_p", [2, 128, 8 * 64])
        self.o["rw_s"] = self.dout("o_rw_s", [2, 8, 128, NSS * 64])
        self.o["sh_p"] = self.dout("o_sh_p", [2, 128, 8])
        self.o["sh_s"] = self.dout("o_sh_s", [2, 128, 8 * NSS])
        self.d_lnp = self.din("lnp", [128, 128])
        self.d_wq = self.din("peer_wq", [DEPTH, D, 2048])
        self.d_keysT = self.din("peer_keysT", [DEPTH, 128, 2048])
        if not (self.debug or {}).get("nopeer"):
            self.d_uT = self.din("peer_uT", [DEPTH, D, 16384])
            self.d_v = self.din("peer_v", [DEPTH, 16384, D])
        self.o_yT = self.dout("o_yT", [D, TP + NSS * TS])
        o_gla_p = self.dout("o_gla_p", [2, 128, 2 * 128])
        o_gla_s = self.dout("o_gla_s", [2, 128, NSS * 2 * 128])
        o_h_p = self.dout("o_h_p", [2, 128, 4])
        o_h_s = self.dout("o_h_s", [2, 128, 4 * NSS])
        o_conv_p = self.dout("o_conv_p", [2, 128, 4 * 3])
        o_conv_s = self.dout("o_conv_s", [2, 128, 4 * NSS * 3])
        self.o.update(gla_p=o_gla_p, gla_s=o_gla_s, h_p=o_h_p, h_s=o_h_s, conv_p=o_conv_p, conv_s=o_conv_s)

        self.ps = [Tl(nc.alloc_psum_tensor("ps%d" % i, [128, 512], F32).ap()) for i in range(8)]
        self.cst = sb("cst", [128, 2816])
        fw.dma(fw.sp, self.cst, cst)
        self.ident = self.cst[:, 0:128]
        self.m_incl = self.cst[:, 128:256]
        self.m_strict = self.cst[:, 256:384]
        self.m_sincl = self.cst[:, 384:512]
        self.m_sstrict = self.cst[:, 512:640]
        self.ones = self.cst[:, 640:768]
        self.identb = sb("identb", [128, 128], BF16)
        fw.cp(self.identb, self.ident)
        self.onesb = sb("onesb", [128, 128], BF16)
        fw.cp(self.onesb, self.ones)

        self.x = sb("x", [128, 8, 128])
        self.xb = sb("xb", [128, 8, 128], BF16)

        self.gla_S = [sb("glaS%d" % j, [128, 2, 128]) for j in range(2)]
        self.lru_h = [sb("lruh%d" % j, [128, 4, 1]) for j in range(2)]
        self.lru_cv = [sb("lrucv%d" % j, [128, 4, 1, 3]) for j in range(2)]
        for j in range(2):
            fw.memset(self.gla_S[j], 0.0)
            fw.memset(self.lru_h[j], 0.0)
            fw.memset(self.lru_cv[j], 0.0)
        fw.arena_init(122880)
        A = self.A = {}
        self.w_in = fw.av(0, [128, 8, MIX_IN], BF16)
        self.w_out = fw.av(41216, [128, 8, D], BF16)
        A["Sb"] = fw.av(57600, [128, NSS, 4, 128], BF16)
        A["glaSs"] = fw.av(73984, [128, NSS, 2, 128], F32)
        A["wq"] = fw.av(65536, [128, 8, 2048], BF16)
        A["CT"] = fw.av(32768, [128, 128, 128], BF16)
        A["UT"] = [fw.av(65536 + 8192 * k, [128, 8, 512], BF16) for k in range(2)]
        A["V"] = [fw.av(81920 + 8192 * k, [128, 4, D], BF16) for k in range(2)]
        A["X"] = fw.av(98304, [128, 32, 128], BF16)
        A["J"] = fw.av(106496, [128, 32, 128], BF16)
        A["s_sb"] = fw.av(0, [128, 16, 128], F32)
        A["s2"] = fw.av(8192, [128, 16, 128], F32)
        A["qT"] = fw.av(16384, [128, 16, 128], F32)
        A["hm"] = fw.av(24576, [128, 8, 128], F32)
        A["hv"] = fw.av(28672, [128, 8, 128], F32)
        A["ht"] = fw.av(114688, [128, 8, 128], F32)
        A["ysb"] = fw.av(118784, [128, 1024], F32)
        R = self.R = {}
        R["w_r"] = fw.av(0, [128, 8, D], BF16)
        R["w_k"] = fw.av(16384, [128, 8, D], BF16)
        R["w_v"] = fw.av(32768, [128, 8, D], BF16)
        R["l1"] = fw.av(49152, [128, 8, 288], BF16)
        R["w2"] = fw.av(53760, [128, D], BF16)
        R["a2"] = fw.av(55808, [128, D], BF16)
        R["v2"] = fw.av(57856, [128, D], BF16)
        R["g2"] = fw.av(59904, [128, D], BF16)
        R["xi"] = [fw.av(61952 + 2048 * k, [128, 8, 128], BF16) for k in range(6)]
        R["xx"] = fw.av(74240, [128, 8, 128], F32)
        R["STs"] = fw.av(78336, [128, NSS, 64], F32)
        R["STb"] = fw.av(82432, [128, NSS, 64], BF16)
        R["Sz"] = fw.av(84480, [128, NSS, 2, 64], BF16)
        R["Apad"] = fw.av(88576, [128, NSS, 64], BF16)
        R["Rpad"] = fw.av(90624, [128, NSS, 64], BF16)
        R["Apad2"] = fw.av(112640, [128, NSS, 64], BF16)
        R["bGpad"] = fw.av(92672, [128, NSS, 128], BF16)
        R["kGpad"] = fw.av(96768, [128, NSS, 128], BF16)
        for k, nm in enumerate(("r32", "k32", "v32", "a32", "g32", "lw", "kk", "k2", "bv", "cum", "cp", "E", "E3", "t", "t2", "y32", "gate")):
            R[nm] = fw.av(100864 + 512 * k, [128, 128], F32)
        self.gla_Ss = [A["glaSs"], A["glaSs"]]
        self.rw_S = [sb("rwS%d" % j, [128, 8, 64]) for j in range(2)]
        self.sh_p = [sb("shp%d" % j, [128, 8, 1]) for j in range(2)]
        self.sh_s = [sb("shs%d" % j, [128, 8, NSS]) for j in range(2)]
        self.vfirst = sb("vfirst", [128, 8, 128])
        self.odv = sb("odv", [128, 112])
        for j in range(2):
            fw.memset(self.rw_S[j], 0.0)
            fw.memset(self.sh_p[j], 0.0)
            fw.dma(fw.sp, self.sh_s[j], self.d_st_sh[j].rearrange("p (c s) -> p c s", c=8))
        self.lru_hs = [sb("lruhs%d" % j, [128, 4, NSS]) for j in range(2)]
        self.lru_cvs = [sb("lrucvs%d" % j, [128, 4, NSS, 3]) for j in range(2)]
        for j in range(2):
            fw.dma(fw.sp, self.lru_hs[j], self.d_st_h[j].rearrange("p (c s) -> p c s", c=4))
            fw.dma(fw.sp, self.lru_cvs[j], self.d_st_conv[j].rearrange("p (c s i) -> p c s i", c=4, s=NSS))

        self.w_gate = sb("w_gate", [16, 256], BF16)
        self.lnp = sb("lnp_sb", [128, 128])
        fw.dma(fw.sp, self.lnp, self.d_lnp)
        self.keysT = sb("keysT_sb", [128, 2048])
        self.lruw = sb("lruw", [128, 2, 4, 128], BF16)
        self.evv = sb("evv", [128, 40])
        self.c8 = [sb("c8_%d" % j, [128, 4]) for j in range(2)]
        self.negevv = sb("negevv", [128, 40])
        self.negodv = sb("negodv", [128, 112])
        self.work_init()

        tiles = [("p", 128 * k, 128) for k in range(16)] + [("p", 2048, 16), ("s", 0, 64)]
        if self.debug:
            tiles = self.debug.get("tiles", tiles)
        layers = (self.debug or {}).get("layers", [0, 1, 2, 3])
        self.precast_all()
        for (kind, t0, n) in tiles:
            src = xT_p if kind == "p" else xT_s
            for c in range(8):
                fw.dma(fw.sp, self.x[:, c, 0:n], src[c * 128:(c + 1) * 128, t0:t0 + n])
            fw.cp(self.xb[:, :, 0:n], self.x[:, :, 0:n])
            for layer in layers:
                j = layer // 2
                if layer % 2 == 0:
                    self.load_even_weights(j)
                    if kind == "s":
                        fw.dma(fw.sp, self.gla_Ss[j], self.d_st_gla[j].rearrange("p (s h v) -> p s h v", s=NSS, h=2))
                    self.gla_lru_tile(j, kind, n)
                    if kind == "s":
                        fw.dma(fw.sp, self.o["gla_s"][j].rearrange("p (s h v) -> p s h v", s=NSS, h=2), self.gla_Ss[j], is_output=True)
                    self.out_proj(n, self.w_out)
                else:
                    self.rwkv_tile(j, kind, n)
                    self.out_proj(n, self.R["w_r"])
                self.ln_tile(n, layer, 0)
                if (self.debug or {}).get("nopeer"):
                    continue
                self.peer_tile(layer, n)
                self.ln_tile(n, layer, 1)
            o0 = t0 if kind == "p" else TP
            for c in range(8):
                fw.dma(fw.sp, self.o_yT[c * 128:(c + 1) * 128, o0:o0 + n], self.x[:, c, 0:n], is_output=True)
        for j in range(2):
            fw.dma(fw.sp, self.o["gla_p"][j].rearrange("p (h v) -> p h v", h=2), self.gla_S[j], is_output=True)
            fw.dma(fw.sp, self.o["rw_p"][j].rearrange("p (c v) -> p c v", c=8), self.rw_S[j], is_output=True)
            fw.dma(fw.sp, self.o["sh_p"][j].rearrange("p (c o) -> p c o", o=1), self.sh_p[j], is_output=True)
            fw.dma(fw.sp, self.o["sh_s"][j].rearrange("p (c s) -> p c s", c=8), self.sh_s[j], is_output=True)
            fw.dma(fw.sp, self.o["h_p"][j].rearrange("p (c o) -> p c o", o=1), self.lru_h[j], is_output=True)
            fw.dma(fw.sp, self.o["h_s"][j].rearrange("p (c s) -> p c s", c=4), self.lru_hs[j], is_output=True)
            fw.dma(fw.sp, self.o["conv_p"][j].rearrange("p (c o i) -> p c o i", c=4, o=1), self.lru_cv[j], is_output=True)
            fw.dma(fw.sp, self.o["conv_s"][j].rearrange("p (c s i) -> p c s i", c=4, s=NSS), self.lru_cvs[j], is_output=True)
        fw.finish()
        return nc

    def load_even_weights(self, j):
        fw = self.fw
        fw.dma(fw.sp, self.w_in, self.sc_["w_in%d" % j].re("(c p) n -> p c n", p=128))
        fw.dma(fw.sp, self.w_out, self.sc_["w_out%d" % j].re("(c p) n -> p c n", p=128))
        fw.dma(fw.pool, self.w_gate, self.d_ev_w_gate[j])
        fw.dma(fw.pool, self.lruw, self.d_ev_lruw[j].rearrange("g c i o -> i g c o"))
        fw.dma(fw.sp, self.evv, self.d_ev_vecs[j])
        fw.actv(self.c8[j], self.evv[:, 31:35], AF.Exp, scale=-1.0)
        fw.actv(self.c8[j], self.c8[j], AF.Ln, bias=self.cst[:, 640:641])
        fw.ts(self.c8[j], self.c8[j], -8.0, None, ALU.mult)
        fw.ts(self.negevv, self.evv, -1.0, None, ALU.mult)

    def out_proj(self, n, w_out):
        fw, W = self.fw, self.W
        for oc in range(8):
            ps = self.psum()
            for ic in range(8):
                fw.mm(ps[:, 0:n], w_out[:, ic, oc * 128:(oc + 1) * 128], W["mix"][:, ic, 0:n], start=(ic == 0), stop=(ic == 7))
            fw.stt(self.x[:, oc, 0:n], self.x[:, oc, 0:n], DN_ALPHA, ps[:, 0:n], ALU.mult, ALU.add)

    def ln_tile(self, n, layer, which):
        fw, W, x = self.fw, self.W, self.x
        fw.tt(W["xsq"][:, :, 0:n], x[:, :, 0:n], x[:, :, 0:n], ALU.mult)
        pm = self.psum()
        for c in range(8):
            fw.mm(pm[:, 0:n], self.ones, x[:, c, 0:n], start=(c == 0), stop=(c == 7))
        pq = self.psum()
        for c in range(8):
            fw.mm(pq[:, 0:n], self.ones, W["xsq"][:, c, 0:n], start=(c == 0), stop=(c == 7))
        mean, rstd, t1 = W["mean"], W["rstd"], W["t1"]
        fw.actv(mean[:, 0:n], pm[:, 0:n], AF.Copy, scale=1.0 / D)
        fw.actv(rstd[:, 0:n], pq[:, 0:n], AF.Copy, scale=1.0 / D)
        fw.tt(t1[:, 0:n], mean[:, 0:n], mean[:, 0:n], ALU.mult)
        fw.tt(rstd[:, 0:n], rstd[:, 0:n], t1[:, 0:n], ALU.subtract)
        fw.actv(rstd[:, 0:n], rstd[:, 0:n], AF.Ln, bias=self.cst[:, 896:897])
        fw.actv(rstd[:, 0:n], rstd[:, 0:n], AF.Exp, scale=-0.5)
        fw.tt(x[:, :, 0:n], x[:, :, 0:n], mean[:, 0:n].re("p (o n) -> p o n", o=1).bc([128, 8, n]), ALU.subtract)
        fw.tt(x[:, :, 0:n], x[:, :, 0:n], rstd[:, 0:n].re("p (o n) -> p o n", o=1).bc([128, 8, n]), ALU.mult)
        col = (layer * 2 + which) * 8
        for c in range(8):
            fw.ts(x[:, c, 0:n], x[:, c, 0:n], self.lnp[:, col + c:col + c + 1], self.lnp[:, 64 + col + c:64 + col + c + 1], ALU.mult, ALU.add)
        fw.cp(self.xb[:, :, 0:n], x[:, :, 0:n])

    def sigm(self, out, in_, negbias=None):
        fw = self.fw
        fw.actv(out, in_, AF.Exp, bias=negbias, scale=-1.0)
        fw.ts(out, out, 1.0, None, ALU.add)
        self.vop("reciprocal", [out], [out], out=out, in_=out)

    def vop(self, name, reads, writes, **kw):
        fw = self.fw
        fw.op(fw.dve, lambda: getattr(self.nc.vector, name)(**{k: _ap(v) for k, v in kw.items()}), reads=reads, writes=writes)

    def precast_all(self):
        fw, nc = self.fw, self.nc
        self.sc_ = {}

        def pc(key, src, rows, cols, step):
            t = Tl(nc.dram_tensor("sc_" + key, [rows, cols], BF16, kind="Internal").ap())
            for r0 in range(0, rows, step):
                fw.dma(fw.pool, t[r0:r0 + step, :], src[r0:r0 + step, :])
            self.sc_[key] = t
        self.uTb, self.vbb = [None] * DEPTH, [None] * DEPTH
        for layer in range(DEPTH):
            j = layer // 2
            if layer % 2 == 0:
                pc("w_in%d" % j, self.d_ev_w_in[j], D, MIX_IN, 256)
                pc("w_out%d" % j, self.d_ev_w_out[j], D, D, 512)
            else:
                for nm in ("w_r", "w_k", "w_v", "w_o"):
                    pc("%s%d" % (nm, j), self.d_od_w[nm][j], D, D, 512)
            pc("wq%d" % layer, self.d_wq[layer], D, 2048, 256)
            if not (self.debug or {}).get("nopeer"):
                pc("uT%d" % layer, self.d_uT[layer], D, 16384, 128)
                pc("v%d" % layer, self.d_v[layer], 16384, D, 2048)
                self.uTb[layer], self.vbb[layer] = self.sc_["uT%d" % layer], self.sc_["v%d" % layer]

    def peer_precast(self):
        fw, nc = self.fw, self.nc
        self.uTb = [Tl(nc.dram_tensor("uTb%d" % l, [D, 16384], BF16, kind="Internal").ap()) for l in range(DEPTH)]
        self.vbb = [Tl(nc.dram_tensor("vbb%d" % l, [16384, D], BF16, kind="Internal").ap()) for l in range(DEPTH)]
        for l in range(DEPTH):
            for k in range(8):
                fw.dma(fw.pool, self.uTb[l][k * 128:(k + 1) * 128, :], self.d_uT[l, k * 128:(k + 1) * 128, :])
            for k in range(8):
                fw.dma(fw.pool, self.vbb[l][k * 2048:(k + 1) * 2048, :], self.d_v[l, k * 2048:(k + 1) * 2048, :])

    def peer_load_u(self, layer, g):
        src = self.uTb[layer].re("(dc p) e -> p dc e", p=128)[:, :, g * 512:(g + 1) * 512]
        self.fw.dma(self.fw.sp, self.A["UT"][g % 2], src)

    def peer_load_v(self, layer, g):
        src = self.vbb[layer][g * 512:(g + 1) * 512, :].re("(c p) d -> p c d", p=128)
        self.fw.dma(self.fw.sp, self.A["V"][g % 2], src)

    def peer_tile(self, layer, n):
        fw, W, A, xb = self.fw, self.W, self.A, self.xb
        wq, qT, s_sb, s2, CT = A["wq"], A["qT"], A["s_sb"], A["s2"], A["CT"]
        fw.dma(fw.sp, wq, self.sc_["wq%d" % layer].re("(c p) n -> p c n", p=128))
        fw.dma(fw.sp, self.keysT, self.d_keysT[layer])
        for ch in range(16):
            ps = self.psum()
            for c in range(8):
                fw.mm(ps[:, 0:n], wq[:, c, ch * 128:(ch + 1) * 128], xb[:, c, 0:n], start=(c == 0), stop=(c == 7))
            fw.cp(qT[:, ch, 0:n], ps[:, 0:n], eng=fw.act)
        for g_ in range(2):
            self.peer_load_u(layer, g_)
            self.peer_load_v(layer, g_)
        fw.tt(s2[:, :, 0:n], qT[:, :, 0:n], qT[:, :, 0:n], ALU.mult)
        hm, hv, ht = W["hm"], W["hv"], W["ht"]
        for h in range(8):
            pm = self.psum()
            fw.mm(pm[:, 0:n], self.ones, qT[:, 2 * h, 0:n], start=True, stop=False)
            fw.mm(pm[:, 0:n], self.ones, qT[:, 2 * h + 1, 0:n], start=False, stop=True)
            fw.actv(hm[:, h, 0:n], pm[:, 0:n], AF.Copy, scale=1.0 / 256)
            pq = self.psum()
            fw.mm(pq[:, 0:n], self.ones, s2[:, 2 * h, 0:n], start=True, stop=False)
            fw.mm(pq[:, 0:n], self.ones, s2[:, 2 * h + 1, 0:n], start=False, stop=True)
            fw.actv(hv[:, h, 0:n], pq[:, 0:n], AF.Copy, scale=1.0 / 256)
        fw.tt(ht[:, :, 0:n], hm[:, :, 0:n], hm[:, :, 0:n], ALU.mult)
        fw.tt(hv[:, :, 0:n], hv[:, :, 0:n], ht[:, :, 0:n], ALU.subtract)
        fw.actv(hv[:, :, 0:n], hv[:, :, 0:n], AF.Ln, bias=self.cst[:, 896:897])
        fw.actv(hv[:, :, 0:n], hv[:, :, 0:n], AF.Exp, scale=-0.5)
        for half in range(2):
            qv = qT.re("p (h two) n -> p h two n", two=2)[:, :, half, 0:n]
            fw.tt(qv, qv, hm[:, :, 0:n], ALU.subtract)
            fw.tt(qv, qv, hv[:, :, 0:n], ALU.mult)
        for g in range(4):
            ps = self.psum()
            for q in range(4):
                ch = 4 * g + q
                fw.mm(ps[0:n, q * 128:(q + 1) * 128], qT[:, ch, 0:n], self.keysT[:, ch * 128:(ch + 1) * 128])
            fw.cp(s_sb[0:n, 4 * g:4 * g + 4, :], ps[0:n, :].re("p (a k) -> p a k", a=4), eng=fw.act)
        vv, idx = W["vv"], W["idx"]
        for l in range(16):
            sl, s2l = s_sb[0:n, l, :], s2[0:n, l, :]
            self.vop("max", [sl], [vv[0:n, l, 0:8]], out=vv[0:n, l, 0:8], in_=sl)
            self.vop("match_replace", [sl, vv[0:n, l, 0:8]], [s2l], out=s2l, in_to_replace=vv[0:n, l, 0:8], in_values=sl, imm_value=-1e30)
            self.vop("max", [s2l], [vv[0:n, l, 8:16]], out=vv[0:n, l, 8:16], in_=s2l)
            self.vop("max_index", [sl, vv[0:n, l, 0:8]], [idx[0:n, l, 0:8]], out=idx[0:n, l, 0:8], in_max=vv[0:n, l, 0:8], in_values=sl)
            self.vop("max_index", [sl, vv[0:n, l, 8:16]], [idx[0:n, l, 8:16]], out=idx[0:n, l, 8:16], in_max=vv[0:n, l, 8:16], in_values=sl)
        fw.cp(W["idxf"][0:n], idx[0:n])
        cand = s2.re("p a k -> p (a k)").re("p (h c) -> p h c", h=8)
        cand2 = qT.re("p a k -> p (a k)").re("p (h c) -> p h c", h=8)
        vv4 = vv.re("p (h two) k -> p h two k", two=2)
        if4 = W["idxf"].re("p (h two) k -> p h two k", two=2)
        c4 = cand.re("p h (a b) -> p h a b", a=16)
        fw.tt(c4[0:n], vv4[0:n, :, 0, :].re("p h (a o) -> p h a o", o=1).bc([n, 8, 16, 16]),
              vv4[0:n, :, 1, :].re("p h (o b) -> p h o b", o=1).bc([n, 8, 16, 16]), ALU.add)
        sc, pos = W["sc"], W["pos"]
        for h in range(8):
            ch_, c2 = cand[0:n, h, :], cand2[0:n, h, :]
            self.vop("max", [ch_], [sc[0:n, h, 0:8]], out=sc[0:n, h, 0:8], in_=ch_)
            self.vop("match_replace", [ch_, sc[0:n, h, 0:8]], [c2], out=c2, in_to_replace=sc[0:n, h, 0:8], in_values=ch_, imm_value=-1e30)
            self.vop("max", [c2], [sc[0:n, h, 8:16]], out=sc[0:n, h, 8:16], in_=c2)
            self.vop("max_index", [ch_, sc[0:n, h, 0:8]], [pos[0:n, h, 0:8]], out=pos[0:n, h, 0:8], in_max=sc[0:n, h, 0:8], in_values=ch_)
            self.vop("max_index", [ch_, sc[0:n, h, 8:16]], [pos[0:n, h, 8:16]], out=pos[0:n, h, 8:16], in_max=sc[0:n, h, 8:16], in_values=ch_)
        fw.ts(W["pa"][0:n], pos[0:n], 4, None, ALU.logical_shift_right)
        fw.cp(W["paf"][0:n], W["pa"][0:n])
        fw.ts(W["pa"][0:n], pos[0:n], 15, None, ALU.bitwise_and)
        fw.cp(W["pbf"][0:n], W["pa"][0:n])
        eq = cand2.re("p h (a b) -> p h a b", a=16)
        iota16 = self.cst[0:n, 2048:2064].re("p (x y a) -> p x y a", x=1, y=1).bc([n, 8, 16, 16])
        for (pf, half, dst) in ((W["paf"], 0, W["pe1"]), (W["pbf"], 1, W["pe2"])):
            fw.tt(eq[0:n], pf[0:n].re("p h (k o) -> p h k o", o=1).bc([n, 8, 16, 16]), iota16, ALU.is_equal)
            fw.tt(eq[0:n], eq[0:n], if4[0:n, :, half, :].re("p h (o a) -> p h o a", o=1).bc([n, 8, 16, 16]), ALU.mult)
            self.vop("tensor_reduce", [eq[0:n]], [dst[0:n]], out=dst[0:n], in_=eq[0:n], axis=AX.X, op=ALU.add)
        pg, pz = W["pg"], W["pz"]
        fw.tt(pg[0:n], sc[0:n], sc[0:n, :, 0:1].bc([n, 8, 16]), ALU.subtract)
        fw.actv(pg[0:n], pg[0:n], AF.Exp)
        self.vop("tensor_reduce", [pg[0:n]], [pz[0:n]], out=pz[0:n], in_=pg[0:n], axis=AX.X, op=ALU.add)
        self.vop("reciprocal", [pz[0:n]], [pz[0:n]], out=pz[0:n], in_=pz[0:n])
        fw.tt(pg[0:n], pg[0:n], pz[0:n].re("p (h o) -> p h o", o=1).bc([n, 8, 16]), ALU.mult)
        for (src, dst) in ((W["pe1"], W["e1T"]), (W["pe2"], W["e2T"]), (pg, W["gT"])):
            ps = self.psum()
            fw.tr(ps[:, 0:n], src[0:n].re("p h k -> p (h k)"), self.ident[0:n, 0:n])
            fw.cp(dst[:, 0:n], ps[:, 0:n])
        X, J = A["X"], A["J"]
        iota128 = self.cst[:, 2048:2176].re("p (o i) -> p o i", o=1)
        ev = 0
        for t0 in range(0, n, 32):
            nb = min(32, n - t0)
            bc3 = lambda v: v[:, t0:t0 + nb].re("p (t o) -> p t o", o=1).bc([128, nb, 128])
            fw.tt(X[:, 0:nb, :], iota128.bc([128, nb, 128]), bc3(W["e1T"]), ALU.is_equal)
            fw.tt(X[:, 0:nb, :], X[:, 0:nb, :], bc3(W["gT"]), ALU.mult)
            fw.tt(J[:, 0:nb, :], iota128.bc([128, nb, 128]), bc3(W["e2T"]), ALU.is_equal)
            for q0 in range(0, nb, 4):
                k = min(4, nb - q0)
                ps = self.psum()
                for q in range(k):
                    fw.mm(ps[:, q * 128:(q + 1) * 128], J[:, q0 + q, :], X[:, q0 + q, :])
                fw.cp(CT[:, t0 + q0:t0 + q0 + k, :], ps[:, 0:k * 128].re("p (t i) -> p t i", i=128), eng=(fw.act if ev % 2 else fw.dve))
                ev += 1
        y0, y1 = self.ps[6], self.ps[7]

        def h_stage(i):
            g, q = divmod(i, 4)
            ub = A["UT"][g % 2]
            ph = self.psum()
            for c in range(8):
                fw.mm(ph[:, 0:n], ub[:, c, q * 128:(q + 1) * 128], xb[:, c, 0:n], start=(c == 0), stop=(c == 7))
            ha, pT = W["hact"][i % 3], W["pT"][i % 3]
            fw.actv(ha[:, 0:n], ph[:, 0:n], AF.Gelu_apprx_tanh)
            fw.tt(pT[:, 0:n], ha[:, 0:n], CT[:, 0:n, i], ALU.mult)
            if q == 3 and g + 2 < 32:
                self.peer_load_u(layer, g + 2)

        def y_stage(i):
            g, q = divmod(i, 4)
            vb = A["V"][g % 2]
            pT = W["pT"][i % 3]
            fw.mm(y0[0:n, :], pT[:, 0:n], vb[:, q, 0:512], start=(i == 0), stop=(i == 127))
            fw.mm(y1[0:n, :], pT[:, 0:n], vb[:, q, 512:1024], start=(i == 0), stop=(i == 127))
            if q == 3 and g + 2 < 32:
                self.peer_load_v(layer, g + 2)
        h_stage(0)
        h_stage(1)
        for i in range(128):
            if i + 2 < 128:
                h_stage(i + 2)
            y_stage(i)
        ysb = W["ysb"]
        fw.cp(ysb[0:n, 0:512], y0[0:n, :])
        fw.cp(ysb[0:n, 512:1024], y1[0:n, :], eng=fw.act)
        for c in range(8):
            ps = self.psum()
            fw.tr(ps[:, 0:n], ysb[0:n, c * 128:(c + 1) * 128], self.ident[0:n, 0:n])
            fw.stt(self.x[:, c, 0:n], self.x[:, c, 0:n], DN_ALPHA, ps[:, 0:n], ALU.mult, ALU.add)

    def rwkv_tile(self, j, kind, n):
        fw, W, R, x = self.fw, self.W, self.R, self.x
        nseg, L = (1, n) if kind == "p" else (NSS, TS)
        nlev = {128: 6, 16: 3, 4: 1}[L]
        self.nrot = 5
        odv = self.odv
        cst = self.cst
        for nm in ("w_r", "w_k", "w_v"):
            fw.dma(fw.sp, R[nm], self.sc_["%s%d" % (nm, j)].re("(c p) n -> p c n", p=128))
        for c in range(8):
            fw.dma(fw.pool, R["l1"][:, c, :], self.d_od_l1[j, c * 128:(c + 1) * 128, :])
        fw.dma(fw.pool, R["w2"][0:64, :], self.d_od_w2[j])
        fw.dma(fw.pool, R["a2"][0:64, :], self.d_od_a2[j])
        fw.dma(fw.pool, R["v2"][0:32, :], self.d_od_v2[j])
        fw.dma(fw.pool, R["g2"], self.d_od_g2[j])
        fw.dma(fw.sp, odv, self.d_od_vecs[j])
        fw.ts(self.negodv, odv, -1.0, None, ALU.mult)
        sh = self.sh_p[j] if kind == "p" else self.sh_s[j]
        xv_ = x[:, :, 0:n].re("p c (s l) -> p c s l", s=nseg)
        xxv = R["xx"][:, :, 0:n].re("p c (s l) -> p c s l", s=nseg)
        if L > 1:
            fw.tt(xxv[:, :, :, 1:L], xv_[:, :, :, 0:L - 1], xv_[:, :, :, 1:L], ALU.subtract)
        fw.tt(xxv[:, :, :, 0:1], sh.re("p c (s o) -> p c s o", o=1), xv_[:, :, :, 0:1], ALU.subtract)
        fw.cp(sh.re("p c (s o) -> p c s o", o=1), xv_[:, :, :, L - 1:L])
        for i in range(6):
            for c in range(8):
                fw.stt(R["xi"][i][:, c, 0:n], R["xx"][:, c, 0:n], odv[:, i * 8 + c:i * 8 + c + 1], x[:, c, 0:n], ALU.mult, ALU.add)
        xr, xw, xk, xvv, xa, xg = R["xi"]
        l1 = R["l1"]

        def lora1(dst, src, c0, ncol, func):
            ps = self.psum()
            for c in range(8):
                fw.mm(ps[0:ncol, 0:n], l1[:, c, c0:c0 + ncol], src[:, c, 0:n], start=(c == 0), stop=(c == 7))
            fw.actv(dst[0:ncol, 0:n], ps[0:ncol, 0:n], func)
        lora1(W["w1x"], xw, 0, 64, AF.Tanh)
        lora1(W["a1x"], xa, 64, 64, AF.Copy)
        lora1(W["g1x"], xg, 128, 128, AF.Sigmoid)
        if j > 0:
            lora1(W["v1x"], xvv, 256, 32, AF.Copy)

        def pj(w, src, c):
            ps = self.psum()
            for dc in range(8):
                fw.mm(ps[:, 0:n], w[:, dc, c * 128:(c + 1) * 128], src[:, dc, 0:n], start=(dc == 0), stop=(dc == 7))
            return ps
        N_ = slice(0, n)
        rmask = cst[:, 768:896] if kind == "s" else self.ones
        m_strict = (self.m_sstrict if kind == "s" else self.m_strict)[0:n, 0:n]
        m_incl = (self.m_sincl if kind == "s" else self.m_incl)[0:n, 0:n]
        m_low = (cst[:, 2304:2432] if kind == "s" else cst[:, 2176:2304])[0:n, 0:n]
        segmask = cst[:, 1024:2048].re("p (s t) -> p s t", s=NSS)
        segoh = cst[0:n, 912:928].re("p (s o) -> p s o", o=1)
        r32, k32, v32, a32, g32, lw, kk, k2, bv = (R[k_] for k_ in ("r32", "k32", "v32", "a32", "g32", "lw", "kk", "k2", "bv"))
        cum, cpv, E, E3, t, t2, y32, gate = (R[k_] for k_ in ("cum", "cp", "E", "E3", "t", "t2", "y32", "gate"))
        for c in range(8):
            col = lambda base: odv[:, base + c:base + c + 1]
            fw.cp(r32[:, N_], pj(R["w_r"], xr, c)[:, N_], eng=fw.act)
            fw.cp(k32[:, N_], pj(R["w_k"], xk, c)[:, N_], eng=fw.act)
            fw.cp(v32[:, N_], pj(R["w_v"], xvv, c)[:, N_], eng=fw.act)
            if j == 0:
                fw.cp(self.vfirst[:, c, N_], v32[:, N_])
            else:
                ps = self.psum()
                fw.mm(ps[:, N_], R["v2"][0:32, c * 128:(c + 1) * 128], W["v1x"][0:32, N_])
                self.sigm(gate[:, N_], ps[:, N_], self.negodv[:, 104 + c:105 + c])
                fw.tt(t[:, N_], self.vfirst[:, c, N_], v32[:, N_], ALU.subtract)
                fw.tt(t[:, N_], t[:, N_], gate[:, N_], ALU.mult)
                fw.tt(v32[:, N_], v32[:, N_], t[:, N_], ALU.add)
            fw.cp(W["vbf"][:, N_], v32[:, N_])
            ps = self.psum()
            psb = ps.bitcast(BF16)
            fw.tr(psb[0:n, 0:128], W["vbf"][:, N_], self.identb)
            fw.cp(W["Vtm"][0:n, :], psb[0:n, 0:128])
            ps = self.psum()
            fw.mm(ps[:, N_], R["w2"][0:64, c * 128:(c + 1) * 128], W["w1x"][0:64, N_])
            fw.ts(t[:, N_], ps[:, N_], col(48), -1.0, ALU.add, ALU.mult)
            fw.actv(t[:, N_], t[:, N_], AF.Exp)
            fw.actv(t[:, N_], t[:, N_], AF.Ln, bias=cst[:, 640:641])
            fw.actv(t[:, N_], t[:, N_], AF.Exp, bias=cst[:, 897:898], scale=-1.0)
            fw.ts(lw[:, N_], t[:, N_], -1.0, None, ALU.mult)
            ps = self.psum()
            fw.mm(ps[:, N_], R["a2"][0:64, c * 128:(c + 1) * 128], W["a1x"][0:64, N_])
            self.sigm(a32[:, N_], ps[:, N_], self.negodv[:, 56 + c:57 + c])
            ps = self.psum()
            fw.mm(ps[:, N_], R["g2"][:, c * 128:(c + 1) * 128], W["g1x"][:, N_])
            fw.cp(g32[:, N_], ps[:, N_], eng=fw.act)
            fw.ts(kk[:, N_], k32[:, N_], col(64), None, ALU.mult)
            fw.tt(t[:, N_], kk[:, N_], kk[:, N_], ALU.mult)
            ps = self.psum()
            fw.mm(ps[:, N_], cst[:, 2432:2560], t[:, N_])
            fw.actv(t[:, N_], ps[:, N_], AF.Ln, bias=cst[:, 896:897])
            fw.actv(t[:, N_], t[:, N_], AF.Exp, scale=-0.5)
            fw.tt(kk[:, N_], kk[:, N_], t[:, N_], ALU.mult)
            fw.ts(t[:, N_], a32[:, N_], -1.0, col(72), ALU.add, ALU.mult)
            fw.stt(k2[:, N_], t[:, N_], 1.0, k32[:, N_], ALU.add, ALU.mult)
            fw.tt(bv[:, N_], kk[:, N_], a32[:, N_], ALU.mult)
            fw.scan(cum[:, N_], rmask[:, N_], lw[:, N_], 0.0)
            fw.tt(cpv[:, N_], cum[:, N_], lw[:, N_], ALU.subtract)
            cumv = cum[:, N_].re("p (s l) -> p s l", s=nseg)
            cumC = cumv[:, :, L - 1:L]
            AR = W["AR"]
            fw.actv(E[:, N_], cpv[:, N_], AF.Exp)
            fw.stt(AR[:, 0, N_], kk[:, N_], -1.0, E[:, N_], ALU.mult, ALU.mult)
            fw.actv(E[:, N_], cum[:, N_], AF.Exp)
            fw.tt(AR[:, 1, N_], r32[:, N_], E[:, N_], ALU.mult)
            fw.actv(E[:, N_], cum[:, N_], AF.Exp, scale=-1.0)
            fw.tt(W["Bt"][:, N_], bv[:, N_], E[:, N_], ALU.mult)
            fw.tt(W["Kt"][:, N_], k2[:, N_], E[:, N_], ALU.mult)
            fw.tt(E3[:, N_].re("p (s l) -> p s l", s=nseg), cumC.bc([128, nseg, L]), cumv, ALU.subtract)
            fw.actv(E3[:, N_], E3[:, N_], AF.Exp)
            fw.tt(W["bG"][:, N_], bv[:, N_], E3[:, N_], ALU.mult)
            fw.tt(W["kG"][:, N_], k2[:, N_], E3[:, N_], ALU.mult)
            fw.actv(W["gam"][:, 0:nseg], cumC.re("p s o -> p (s o)"), AF.Exp)
            for hp in range(2):
                hm_ = cst[:, 900 + hp:901 + hp]
                fw.ts(W["Az"][:, hp, N_], AR[:, 0, N_], hm_, None, ALU.mult)
                fw.ts(W["Bz"][:, hp, N_], W["Bt"][:, N_], hm_, None, ALU.mult)
                fw.ts(W["Kz"][:, hp, N_], W["Kt"][:, N_], hm_, None, ALU.mult)
            for (src, dst) in ((W["bG"], W["bGtm"]), (W["kG"], W["kGtm"])):
                ps = self.psum()
                psb = ps.bitcast(BF16)
                fw.tr(psb[0:n, 0:128], src[:, N_], self.identb)
                fw.cp(dst[0:n, :], psb[0:n, 0:128])
            if kind == "p":
                ST = self.rw_S[j][:, c, :].re("p (s v) -> p s v", s=1)
            else:
                ST = R["STs"]
                fw.dma(fw.sp, ST, self.d_st_rw[j, c].rearrange("p (s v) -> p s v", s=NSS))
            STb, Sz = R["STb"], R["Sz"]
            fw.cp(STb[:, 0:nseg, :], ST)
            for hp in range(2):
                fw.ts(Sz[:, 0:nseg, hp, :], ST, cst[:, 900 + hp:901 + hp], None, ALU.mult)
            psY = self.ps[6]
            psS = [self.ps[7], self.ps[5]]
            if kind == "s":
                fw.tt(R["Rpad"][:, :, N_], AR[:, 1:2, N_].bc([128, NSS, n]), segmask, ALU.mult)
                fw.tt(R["bGpad"][0:n], W["bGtm"][0:n, :].re("p (o k) -> p o k", o=1).bc([n, NSS, 128]), segoh.bc([n, NSS, 128]), ALU.mult)
                fw.tt(R["kGpad"][0:n], W["kGtm"][0:n, :].re("p (o k) -> p o k", o=1).bc([n, NSS, 128]), segoh.bc([n, NSS, 128]), ALU.mult)

            def chain(hp):
                hc = slice(hp * 64, hp * 64 + 64)
                H = W["hd"][hp]
                banks = (self.ps[0], self.ps[1]) if hp == 0 else (self.ps[2], self.ps[3])
                cnt = [0]

                def pb_():
                    cnt[0] += 1
                    return banks[cnt[0] % 2]
                Az, Bz, Kz = W["Az"][:, hp, N_], W["Bz"][:, hp, N_], W["Kz"][:, hp, N_]
                ps1 = pb_()
                fw.mm(ps1[0:n, 0:2 * n].re("p (w t) -> p w t", w=2), Bz, AR[:, :, N_])
                fw.tt(H["MT"][0:n, N_], ps1[0:n, 0:n], m_strict, ALU.mult)
                fw.tt(H["MrT"][0:n, N_], ps1[0:n, n:2 * n], m_incl, ALU.mult)
                yield
                ps2 = pb_()
                fw.mm(ps2[0:n, 0:2 * n].re("p (w t) -> p w t", w=2), Kz, AR[:, :, N_])
                fw.tt(H["NT"][0:n, N_], ps2[0:n, 0:n], m_strict, ALU.mult)
                fw.tt(H["NrT"][0:n, N_], ps2[0:n, n:2 * n], m_incl, ALU.mult)
                yield
                ps3 = pb_()
                fw.mm(ps3[0:n, N_], Az, W["Bt"][:, N_])
                P, PT, Pn, PTn, TT = H["Pa"], H["PTa"], H["Pb"], H["PTb"], H["TT"]
                fw.tt(P[0:n, N_], ps3[0:n, N_], m_low, ALU.mult)
                fw.cp(PT[0:n, N_], H["MT"][0:n, N_], eng=fw.act)
                fw.tt(TT[0:n, N_], H["MT"][0:n, N_], self.identb[0:n, 0:n], ALU.add)
                yield
                for lev in range(nlev):
                    pa = pb_()
                    fw.mm(pa[0:n, N_], PT[0:n, N_], P[0:n, N_])
                    pb = pb_()
                    fw.mm(pb[0:n, N_], P[0:n, N_], PT[0:n, N_])
                    fw.cp(Pn[0:n, N_], pa[0:n, N_])
                    fw.cp(PTn[0:n, N_], pb[0:n, N_], eng=fw.act)
                    yield
                    pt = pb_()
                    fw.mm(pt[0:n, N_], Pn[0:n, N_], TT[0:n, N_])
                    fw.tt(TT[0:n, N_], TT[0:n, N_], pt[0:n, N_], ALU.add)
                    P, PT, Pn, PTn = Pn, PTn, P, PT
                    yield
                psW = pb_()
                if kind == "p":
                    fw.mm(psW[0:n, 0:64], Az, STb[:, 0, :], start=True, stop=False)
                else:
                    Apad = R["Apad"] if hp == 0 else R["Apad2"]
                    fw.tt(Apad[:, :, N_], W["Az"][:, hp:hp + 1, N_].bc([128, NSS, n]), segmask, ALU.mult)
                    for s_ in range(NSS):
                        fw.mm(psW[0:n, 0:64], Apad[:, s_, N_], STb[:, s_, :], start=(s_ == 0), stop=False)
                fw.mm(psW[0:n, 0:64], H["NT"][0:n, N_], W["Vtm"][0:n, hc], start=False, stop=True)
                fw.cp(H["Wsb"][0:n, :], psW[0:n, 0:64])
                yield
                psZ = pb_()
                fw.mm(psZ[0:n, 0:64], TT[0:n, N_], H["Wsb"][0:n, :])
                fw.cp(H["Zsb"][0:n, :], psZ[0:n, 0:64], eng=fw.act)
                yield
                if kind == "p":
                    fw.mm(psY[hc, N_], Sz[:, 0, hp, :], AR[:, 1, N_], start=True, stop=False)
                else:
                    for s_ in range(NSS):
                        fw.mm(psY[hc, N_], Sz[:, s_, hp, :], R["Rpad"][:, s_, N_], start=(s_ == 0), stop=False)
                fw.mm(psY[hc, N_], H["Zsb"][0:n, :], H["MrT"][0:n, N_], start=False, stop=False)
                fw.mm(psY[hc, N_], W["Vtm"][0:n, hc], H["NrT"][0:n, N_], start=False, stop=True)
                if kind == "p":
                    fw.mm(psS[0][hc, 0:64], W["bGtm"][0:n, hc], H["Zsb"][0:n, :], start=True, stop=False)
                    fw.mm(psS[0][hc, 0:64], W["kGtm"][0:n, hc], W["Vtm"][0:n, hc], start=False, stop=True)
                else:
                    for s_ in range(NSS):
                        o_ = psS[s_ // 8][hc, (s_ % 8) * 64:(s_ % 8 + 1) * 64]
                        fw.mm(o_, R["bGpad"][0:n, s_, hc], H["Zsb"][0:n, :], start=True, stop=False)
                        fw.mm(o_, R["kGpad"][0:n, s_, hc], W["Vtm"][0:n, hc], start=False, stop=True)
            gens = [chain(0), chain(1)]
            while gens:
                for g_ in list(gens):
                    try:
                        next(g_)
                    except StopIteration:
                        gens.remove(g_)
            if kind == "p":
                fw.stt(ST[:, 0, :], ST[:, 0, :], W["gam"][:, 0:1], psS[0][:, 0:64], ALU.mult, ALU.add)
            else:
                fw.tt(ST, ST, W["gam"][:, 0:NSS].re("p (s o) -> p s o", o=1).bc([128, NSS, 64]), ALU.mult)
                for g_ in range(2):
                    fw.tt(ST[:, 8 * g_:8 * g_ + 8, :], ST[:, 8 * g_:8 * g_ + 8, :], psS[g_][:, :].re("p (s v) -> p s v", s=8), ALU.add)
                fw.dma(fw.sp, self.o["rw_s"][j, c].rearrange("p (s v) -> p s v", s=NSS), ST, is_output=True)
            fw.cp(y32[:, N_], psY[:, N_])
            pm = self.psum()
            fw.mm(pm[:, N_], cst[:, 2560:2688], y32[:, N_])
            fw.tt(t[:, N_], y32[:, N_], y32[:, N_], ALU.mult)
            pq = self.psum()
            fw.mm(pq[:, N_], cst[:, 2560:2688], t[:, N_])
            fw.cp(t2[:, N_], pm[:, N_], eng=fw.act)
            fw.tt(y32[:, N_], y32[:, N_], t2[:, N_], ALU.subtract)
            fw.tt(t2[:, N_], t2[:, N_], t2[:, N_], ALU.mult)
            fw.tt(t2[:, N_], pq[:, N_], t2[:, N_], ALU.subtract)
            fw.actv(t2[:, N_], t2[:, N_], AF.Ln, bias=cst[:, 898:899])
            fw.actv(t2[:, N_], t2[:, N_], AF.Exp, scale=-0.5)
            fw.tt(y32[:, N_], y32[:, N_], t2[:, N_], ALU.mult)
            fw.ts(y32[:, N_], y32[:, N_], col(88), col(96), ALU.mult, ALU.add)
            fw.stt(t[:, N_], r32[:, N_], col(80), k2[:, N_], ALU.mult, ALU.mult)
            pr = self.psum()
            fw.mm(pr[:, N_], cst[:, 2432:2560], t[:, N_])
            fw.tt(t[:, N_], pr[:, N_], v32[:, N_], ALU.mult)
            fw.tt(y32[:, N_], y32[:, N_], t[:, N_], ALU.add)
            fw.tt(W["mix"][:, c, N_], y32[:, N_], g32[:, N_], ALU.mult)
        self.nrot = 6
        fw.dma(fw.sp, R["w_r"], self.sc_["w_o%d" % j].re("(c p) n -> p c n", p=128))

    def work_init(self):
        fw = self.fw
        sb = self.fw.sb
        W = self.W = {}
        W["qe"] = sb("qe", [128, 2, 128], BF16)
        W["ke"] = sb("ke", [128, 2, 128], BF16)
        W["qepad"] = fw.av(90368, [128, NSS, 64], BF16)
        W["kl"] = sb("kl", [128, 2, 128], BF16)
        W["kltm"] = sb("kltm", [128, 2, 128], BF16)
        W["klpad"] = fw.av(92416, [128, NSS, 128], BF16)
        W["k32"] = fw.av(101696, [128, 2, 128], F32)
        W["vtm"] = fw.av(105792, [128, 512], BF16)
        W["gl"] = sb("gl", [16, 128], BF16)
        W["la"] = fw.av(102720, [128, 2, 128], F32)
        W["cum"] = fw.av(103744, [128, 2, 128], F32)
        W["e1"] = fw.av(104768, [128, 2, 128], F32)
        W["dcy"] = sb("dcy", [128, 2, NSS])
        W["att"] = sb("att", [128, 128], BF16)
        W["osq"] = sb("osq", [128, 128], BF16)
        W["rstd"] = sb("rstd", [128, 128])
        W["sr"] = sb("sr", [128, 128])
        W["on"] = sb("on", [128, 128])
        W["mix"] = fw.av(110592, [128, 8, 128], BF16)
        W["Sb"] = self.A["Sb"]
        W["xsq"] = sb("xsq", [128, 8, 128])
        W["mean"] = sb("mean", [128, 128])
        W["hm"], W["hv"], W["ht"] = self.A["hm"], self.A["hv"], self.A["ht"]
        W["vv"] = sb("vv", [128, 16, 16])
        W["idx"] = sb("idx", [128, 16, 16], U32)
        W["idxf"] = sb("idxf", [128, 16, 16])
        W["sc"] = sb("sc", [128, 8, 16])
        W["pos"] = sb("pos", [128, 8, 16], U32)
        W["pa"] = sb("pa", [128, 8, 16], U32)
        W["paf"] = sb("paf", [128, 8, 16])
        W["pbf"] = sb("pbf", [128, 8, 16])
        W["pe1"] = sb("pe1", [128, 8, 16])
        W["pe2"] = sb("pe2", [128, 8, 16])
        W["pg"] = sb("pg", [128, 8, 16])
        W["pz"] = sb("pz", [128, 8])
        W["e1T"] = sb("e1T", [128, 128])
        W["e2T"] = sb("e2T", [128, 128])
        W["gT"] = sb("gT", [128, 128])
        W["hact"] = [sb("hact%d" % k, [128, 128], BF16) for k in range(3)]
        W["pT"] = [sb("pT%d" % k, [128, 128], BF16) for k in range(3)]
        W["ysb"] = self.A["ysb"]
        for nm in ("Vtm", "bGtm", "kGtm", "Bt", "Kt", "bG", "kG", "vbf", "g1x"):
            W[nm] = sb("h_" + nm, [128, 128], BF16)
        W["hd"] = []
        for hp in range(2):
            Hd = {nm: sb("hd%d_%s" % (hp, nm), [128, 128], BF16) for nm in ("MT", "MrT", "NT", "NrT", "Pa", "Pb", "PTa", "PTb", "TT")}
            Hd["Wsb"] = sb("hd%d_Wsb" % hp, [128, 64], BF16)
            Hd["Zsb"] = sb("hd%d_Zsb" % hp, [128, 64], BF16)
            W["hd"].append(Hd)
        W["AR"] = sb("h_AR", [128, 2, 128], BF16)
        for nm in ("Az", "Bz", "Kz"):
            W[nm] = sb("h_" + nm, [128, 2, 128], BF16)
        W["w1x"] = sb("h_w1x", [64, 128], BF16)
        W["a1x"] = sb("h_a1x", [64, 128], BF16)
        W["v1x"] = sb("h_v1x", [32, 128], BF16)
        W["gam"] = sb("h_gam", [128, NSS])
        W["kez"] = fw.av(106816, [128, 2, 2, 128], BF16)
        W["xcat"] = fw.av(96512, [128, 4, 131], F32)
        W["xc"] = fw.av(98624, [128, 4, 128], F32)
        W["xcb"] = fw.av(100672, [128, 4, 128], BF16)
        W["ga"] = sb("ga", [128, 128])
        W["gx"] = sb("gx", [128, 128])
        W["at"] = sb("at", [128, 128])
        W["bt"] = sb("bt", [128, 128])
        W["hh"] = sb("hh", [128, 128])
        W["gg4"] = fw.av(107840, [128, 4, 128], F32)
        W["t1"] = sb("t1", [128, 128])

    def gla_lru_tile(self, j, kind, n):
        fw = self.fw
        W = self.W
        xb = self.xb
        w_in = self.w_in
        nseg, L = (1, n) if kind == "p" else (NSS, TS)
        evv = self.evv
        S = self.gla_S[j] if kind == "p" else self.gla_Ss[j]

        def proj(col0, ncols, ps):
            for c in range(8):
                fw.mm(ps[0:ncols, 0:n], w_in[:, c, col0:col0 + ncols], xb[:, c, 0:n], start=(c == 0), stop=(c == 7))

        import os
        STG = int(os.environ.get("STG", "99"))
        if STG < 1:
            return
        ps = self.psum()
        proj(1024, 16, ps)
        fw.cp(W["gl"][:, 0:n], ps[0:16, 0:n])
        for c in range(2):
            ps = self.psum()
            fw.mm(ps[:, 0:n], self.w_gate[:, c * 128:(c + 1) * 128], W["gl"][:, 0:n])
            fw.ts(W["t1"][:, 0:n], ps[:, 0:n], evv[:, c:c + 1], -1.0, ALU.add, ALU.mult)
            fw.actv(W["t1"][:, 0:n], W["t1"][:, 0:n], AF.Exp)
            fw.actv(W["la"][:, c, 0:n], W["t1"][:, 0:n], AF.Ln, bias=self.cst[:, 640:641])
        if STG < 2:
            return
        fw.ts(W["la"][:, :, 0:n], W["la"][:, :, 0:n], -1.0 / 16.0, None, ALU.mult)
        rmask = self.cst[:, 768:896] if kind == "s" else self.ones
        for c in range(2):
            fw.scan(W["cum"][:, c, 0:n], rmask[:, 0:n], W["la"][:, c, 0:n], 0.0)
        cumv = W["cum"][:, :, 0:n].re("p c (s l) -> p c s l", s=nseg)
        cumC = cumv[:, :, :, L - 1:L]
        if STG < 3:
            return
        for c in range(2):
            ps = self.psum()
            proj(c * 128, 128, ps)
            fw.actv(W["e1"][:, c, 0:n], W["cum"][:, c, 0:n], AF.Exp)
            fw.stt(W["qe"][:, c, 0:n], ps[:, 0:n], 0.125, W["e1"][:, c, 0:n], ALU.mult, ALU.mult)
            ps = self.psum()
            proj(256 + c * 128, 128, ps)
            fw.cp(W["k32"][:, c, 0:n], ps[:, 0:n])
            fw.actv(W["e1"][:, c, 0:n], W["cum"][:, c, 0:n], AF.Exp, scale=-1.0)
            fw.tt(W["ke"][:, c, 0:n], W["k32"][:, c, 0:n], W["e1"][:, c, 0:n], ALU.mult)
        if STG < 4:
            return
        e1v = W["e1"][:, :, 0:n].re("p c (s l) -> p c s l", s=nseg)
        fw.tt(e1v, cumC.bc([128, 2, nseg, L]), cumv, ALU.subtract)
        fw.actv(W["e1"][:, :, 0:n], W["e1"][:, :, 0:n], AF.Exp)
        fw.tt(W["kl"][:, :, 0:n], W["k32"][:, :, 0:n], W["e1"][:, :, 0:n], ALU.mult)
        fw.actv(W["dcy"][:, :, 0:nseg], cumC.re("p c s o -> p c (s o)"), AF.Exp)
        if STG < 5:
            return
        for c in range(2):
            ps = self.psum()
            psb = ps.bitcast(BF16)
            fw.tr(psb[0:n, 0:128], W["kl"][:, c, 0:n], self.identb)
            fw.cp(W["kltm"][0:n, c, :], psb[0:n, 0:128])
        if STG < 6:
            return
        ps = self.psum()
        for c in range(8):
            fw.mm(ps[0:n, 0:512], xb[:, c, 0:n], w_in[:, c, 512:1024], start=(c == 0), stop=(c == 7))
        fw.cp(W["vtm"][0:n, :], ps[0:n, :], eng=fw.act)
        Sb = W["Sb"]
        Sv = S if kind == "s" else S.re("p (s c) v -> p s c v", s=1)
        for h in range(4):
            fw.ts(Sb[:, 0:nseg, h, :], Sv[:, :, h // 2, :], self.cst[:, 900 + h % 2:901 + h % 2], None, ALU.mult)
        for c in range(2):
            for hp in range(2):
                fw.ts(W["kez"][:, c, hp, 0:n], W["ke"][:, c, 0:n], self.cst[:, 900 + hp:901 + hp], None, ALU.mult)
        if STG < 7:
            return
        mask = self.m_incl if kind == "p" else self.m_sincl
        for h in range(4):
            c, hp = h // 2, h % 2
            pr = slice(hp * 64, hp * 64 + 64)
            ps = self.psum()
            fw.mm(ps[0:n, 0:n], W["kez"][:, c, hp, 0:n], W["qe"][:, c, 0:n])
            fw.tt(W["att"][0:n, 0:n], ps[0:n, 0:n], mask[0:n, 0:n], ALU.mult)
            SUB = float(os.environ.get("SUB", "99"))
            if SUB < 1:
                continue
            po = self.psum()
            fw.mm(po[:, 0:n], W["vtm"][0:n, h * 128:(h + 1) * 128], W["att"][0:n, 0:n], start=True, stop=False)
            if kind == "p":
                fw.mm(po[:, 0:n], Sb[:, 0, h, :], W["qe"][:, c, 0:n], start=False, stop=True)
            else:
                if hp == 0:
                    fw.tt(W["qepad"], W["qe"][:, c:c + 1, 0:n].bc([128, NSS, n]),
                          self.cst[:, 1024:2048].re("p (s t) -> p s t", s=NSS), ALU.mult)
                for s in range(nseg):
                    fw.mm(po[:, 0:n], Sb[:, s, h, :], W["qepad"][:, s, :], start=False, stop=(s == nseg - 1))
            if SUB < 2:
                continue
            fw.actv(W["osq"][:, 0:n], po[:, 0:n], AF.Square)
            pm = self.psum()
            fw.mm(pm[:, 0:n], self.onesb, W["osq"][:, 0:n])
            fw.actv(W["rstd"][:, 0:n], pm[:, 0:n], AF.Ln, bias=self.cst[:, 896:897], scale=1.0 / 128.0)
            fw.actv(W["rstd"][:, 0:n], W["rstd"][:, 0:n], AF.Exp, scale=-0.5)
            fw.stt(W["on"][:, 0:n], po[:, 0:n], evv[:, 2:3], W["rstd"][:, 0:n], ALU.mult, ALU.mult)
            if SUB < 3:
                continue
            pr_ = self.psum()
            proj(1040 + h * 128, 128, pr_)
            self.sigm(W["sr"][:, 0:n], pr_[:, 0:n])
            fw.tt(W["sr"][:, 0:n], W["sr"][:, 0:n], pr_[:, 0:n], ALU.mult)
            fw.tt(W["mix"][:, h, 0:n], W["on"][:, 0:n], W["sr"][:, 0:n], ALU.mult)
        if STG < 8:
            return
        if kind == "p":
            for c in range(2):
                ps = self.psum()
                for hp in range(2):
                    h = c * 2 + hp
                    fw.mm(ps[hp * 64:hp * 64 + 64, 0:128], W["kltm"][0:n, c, hp * 64:hp * 64 + 64], W["vtm"][0:n, h * 128:(h + 1) * 128])
                fw.stt(S[:, c, :], S[:, c, :], W["dcy"][:, c, 0:1], ps[:, 0:128], ALU.mult, ALU.add)
        else:
            for c in range(2):
                fw.tt(W["klpad"][0:n], W["kltm"][0:n, c:c + 1, :].bc([n, NSS, 128]),
                      self.cst[0:n, 896 + 16:896 + 32].re("p (s o) -> p s o", o=1).bc([n, NSS, 128]), ALU.mult)
                for s0 in range(0, NSS, 4):
                    ps = self.psum()
                    for s in range(s0, s0 + 4):
                        for hp in range(2):
                            h = c * 2 + hp
                            fw.mm(ps[hp * 64:hp * 64 + 64, (s - s0) * 128:(s - s0 + 1) * 128], W["klpad"][0:n, s, hp * 64:hp * 64 + 64],
                                  W["vtm"][0:n, h * 128:(h + 1) * 128])
                    for s in range(s0, s0 + 4):
                        fw.stt(S[:, s, c, :], S[:, s, c, :], W["dcy"][:, c, s:s + 1], ps[:, (s - s0) * 128:(s - s0 + 1) * 128], ALU.mult, ALU.add)

        if STG < 9:
            return
        cv = self.lru_cv[j] if kind == "p" else self.lru_cvs[j]
        hst = self.lru_h[j] if kind == "p" else self.lru_hs[j]
        LL = L + 3
        xcat = W["xcat"][:, :, 0:nseg * LL].re("p c (s l) -> p c s l", s=nseg)
        fw.cp(xcat[:, :, :, 0:3], cv)
        for c in range(4):
            ps = self.psum()
            proj(1552 + c * 128, 128, ps)
            fw.cp(xcat[:, c, :, 3:LL], ps[:, 0:n].re("p (s l) -> p s l", s=nseg), eng=fw.act)
        fw.cp(cv, xcat[:, :, :, L:LL])
        for c in range(4):
            xcv = W["xc"][:, c, 0:n].re("p (s l) -> p s l", s=nseg)
            fw.ts(xcv, xcat[:, c, :, 0:L], evv[:, 7 + c:8 + c], evv[:, 3 + c:4 + c], ALU.mult, ALU.add)
            for i in range(1, 4):
                fw.stt(xcv, xcat[:, c, :, i:i + L], evv[:, 7 + 4 * i + c:8 + 4 * i + c], xcv, ALU.mult, ALU.add)
        if STG < 10:
            return
        fw.cp(W["xcb"][:, :, 0:n], W["xc"][:, :, 0:n])
        for c in range(4):
            ps = self.psum()
            proj(2064 + c * 128, 128, ps)
            fw.actv(W["gg4"][:, c, 0:n], ps[:, 0:n], AF.Gelu_apprx_tanh)
        for c in range(4):
            ps = self.psum()
            fw.mm(ps[:, 0:n], self.lruw[:, 0, c, :], W["xcb"][:, c, 0:n])
            self.sigm(W["ga"][:, 0:n], ps[:, 0:n], self.negevv[:, 23 + c:24 + c])
            ps = self.psum()
            fw.mm(ps[:, 0:n], self.lruw[:, 1, c, :], W["xcb"][:, c, 0:n])
            self.sigm(W["gx"][:, 0:n], ps[:, 0:n], self.negevv[:, 27 + c:28 + c])
            fw.actv(W["at"][:, 0:n], W["ga"][:, 0:n], AF.Exp, scale=self.c8[j][:, c:c + 1])
            fw.tt(W["t1"][:, 0:n], W["at"][:, 0:n], W["at"][:, 0:n], ALU.mult)
            fw.ts(W["t1"][:, 0:n], W["t1"][:, 0:n], -1.0, 1.0, ALU.mult, ALU.add)
            fw.actv(W["t1"][:, 0:n], W["t1"][:, 0:n], AF.Ln)
            fw.actv(W["t1"][:, 0:n], W["t1"][:, 0:n], AF.Exp, scale=0.5)
            fw.tt(W["bt"][:, 0:n], W["gx"][:, 0:n], W["xc"][:, c, 0:n], ALU.mult)
            fw.tt(W["bt"][:, 0:n], W["bt"][:, 0:n], W["t1"][:, 0:n], ALU.mult)
            atv = W["at"][:, 0:n].re("p (s l) -> p s l", s=nseg)
            btv = W["bt"][:, 0:n].re("p (s l) -> p s l", s=nseg)
            h0 = hst[:, c, :].re("p (s o) -> p s o", o=1)
            fw.tt(W["t1"][:, 0:nseg].re("p (s o) -> p s o", o=1), atv[:, :, 0:1], h0, ALU.mult)
            fw.tt(btv[:, :, 0:1], btv[:, :, 0:1], W["t1"][:, 0:nseg].re("p (s o) -> p s o", o=1), ALU.add)
            fw.memset(atv[:, :, 0:1], 0.0)
            fw.scan(W["hh"][:, 0:n], W["at"][:, 0:n], W["bt"][:, 0:n], 0.0)
            hv = W["hh"][:, 0:n].re("p (s l) -> p s l", s=nseg)
            fw.cp(h0, hv[:, :, L - 1:L])
            fw.tt(W["mix"][:, 4 + c, 0:n], W["hh"][:, 0:n], W["gg4"][:, c, 0:n], ALU.mult)


def host_consts():
    c = np.zeros((128, 2816), np.float32)
    c[:, 0:128] = np.eye(128)
    s = np.arange(128)[:, None]
    t = np.arange(128)[None, :]
    c[:, 128:256] = (s <= t)
    c[:, 256:384] = (s < t)
    same = (s // 4) == (t // 4)
    c[:, 384:512] = (s <= t) & same
    c[:, 512:640] = (s < t) & same
    c[:, 640:768] = 1.0
    c[:, 768:896] = (np.arange(128) % 4 != 0)[None, :]
    c[:, 896] = LN_EPS
    c[0:64, 900] = 1.0
    c[64:128, 901] = 1.0
    c[:, 912:928] = (np.arange(128)[:, None] // 4 == np.arange(16)[None, :])
    c[:, 2048:2176] = np.arange(128)[None, :]
    c[:, 2176:2304] = (t < s)
    c[:, 2304:2432] = (t < s) & same
    blk = (s // 64) == (t // 64)
    c[:, 2432:2560] = blk
    c[:, 2560:2688] = blk / 64.0
    c[:, 897] = -0.5
    c[:, 898] = 64e-5
    c[:, 1024:2048] = (np.arange(64)[None, :] // 4 == np.arange(16)[:, None]).reshape(1, 1024)
    return c


def _fm(v, nch):
    return np.ascontiguousarray(np.asarray(v, np.float32).reshape(nch, 128).T)


def pack_even(inp):
    vec = np.zeros((2, 128, 40), np.float32)
    lruw = np.zeros((2, 2, 4, 128, 128), np.float32)
    for j in range(2):
        vec[j, :, 0:2] = _fm(inp["ev_gla_b_gate"][j], 2)
        vec[j, :, 2] = inp["ev_gla_norm"][j]
        vec[j, :, 3:7] = _fm(inp["ev_conv_b"][j], 4)
        for i in range(4):
            vec[j, :, 7 + 4 * i:11 + 4 * i] = _fm(inp["ev_conv_w"][j, i], 4)
        vec[j, :, 23:27] = _fm(inp["ev_lru_ba"][j], 4)
        vec[j, :, 27:31] = _fm(inp["ev_lru_bx"][j], 4)
        vec[j, :, 31:35] = _fm(inp["ev_lru_lambda"][j], 4)
        for g, nm in enumerate(("ev_lru_wa", "ev_lru_wx")):
            for c in range(4):
                for nb in range(2):
                    lruw[j, g, c, nb * 64:(nb + 1) * 64, nb * 64:(nb + 1) * 64] = inp[nm][j, 2 * c + nb]
    return vec, lruw


def make_in_maps(inp):
    inp = {k: np.asarray(v) for k, v in inp.items()}
    vec, lruw = pack_even(inp)
    consts = host_consts()
    lnp = np.zeros((128, 128), np.float32)
    lnp[:, 0:64] = inp["ln_g"].reshape(64, 128).T
    lnp[:, 64:128] = inp["ln_b"].reshape(64, 128).T
    keysT = np.ascontiguousarray(inp["peer_keys"].transpose(0, 4, 1, 2, 3).reshape(DEPTH, 128, 2048))
    uT = np.ascontiguousarray(inp["peer_u"].transpose(0, 2, 1))
    v1p = np.zeros((2, D, 32), np.float32)
    v1p[1] = inp["od_v1"][0]
    od_l1 = np.ascontiguousarray(np.concatenate([inp["od_w1"], inp["od_a1"], inp["od_g1"], v1p], axis=2))
    od_v2 = np.zeros((2, 32, D), np.float32)
    od_v2[1] = inp["od_v2"][0]
    od_vecs = np.zeros((2, 128, 112), np.float32)
    for j in range(2):
        for i in range(6):
            od_vecs[j, :, i * 8:(i + 1) * 8] = _fm(inp["od_mu"][j, i], 8)
        for k_, nm in enumerate(("od_w0", "od_a0", "od_k_k", "od_k_a", "od_r_k", "od_lnx_g", "od_lnx_b")):
            od_vecs[j, :, 48 + 8 * k_:56 + 8 * k_] = _fm(inp[nm][j].reshape(-1), 8)
    od_vecs[1, :, 104:112] = _fm(inp["od_v0"][0], 8)
    maps = []
    for c in range(NCORES):
        xp = np.concatenate([inp["meta_tokens"], inp["x_prompt"][c]], axis=0)
        xs = inp["x_sample"][NSS * c:NSS * (c + 1)].reshape(NSS * TS, D)
        sl = slice(NSS * c, NSS * (c + 1))
        g = inp["state_gla"][:, sl].reshape(2, NSS, 2, 2, 64, 128)
        g = g.transpose(0, 3, 4, 1, 2, 5).reshape(2, 128, NSS * 2 * 128)
        hh = inp["state_lru_h"][:, sl].reshape(2, NSS, 4, 128).transpose(0, 3, 2, 1).reshape(2, 128, 4 * NSS)
        cv = inp["state_lru_conv"][:, sl].reshape(2, NSS, 3, 4, 128).transpose(0, 4, 3, 1, 2).reshape(2, 128, 4 * NSS * 3)
        rw = inp["state_rwkv"][:, sl].reshape(2, NSS, 8, 2, 64, 64)
        rw = rw.transpose(0, 2, 3, 5, 1, 4).reshape(2, 8, 128, NSS * 64)
        shh = inp["state_rwkv_shift"][:, sl].reshape(2, NSS, 8, 128).transpose(0, 3, 2, 1).reshape(2, 128, 8 * NSS)
        m = dict(
            xT_p=np.ascontiguousarray(xp.T), xT_s=np.ascontiguousarray(xs.T), consts=consts,
            ev_w_in=inp["ev_w_in"], ev_w_gate=inp["ev_gla_w_gate"], ev_vecs=vec, ev_lruw=lruw,
            ev_w_out=inp["ev_w_out"], st_gla=np.ascontiguousarray(g), st_h=np.ascontiguousarray(hh),
            st_conv=np.ascontiguousarray(cv),
            od_w_r=inp["od_w_r"], od_w_k=inp["od_w_k"], od_w_v=inp["od_w_v"], od_w_o=inp["od_w_o"],
            od_l1=od_l1, od_w2=inp["od_w2"], od_a2=inp["od_a2"], od_v2=od_v2, od_g2=inp["od_g2"], od_vecs=od_vecs,
            st_rw=np.ascontiguousarray(rw), st_sh=np.ascontiguousarray(shh),
            lnp=lnp, peer_wq=inp["peer_w_q"], peer_keysT=keysT, peer_uT=uT, peer_v=inp["peer_v"],
        )
        maps.append(m)
    return maps


def assemble(results):
    out = {}
    p_gla = np.zeros((2, NCORES, 4, 64, 128), np.float32)
    s_gla = np.zeros((2, NCORES * NSS, 4, 64, 128), np.float32)
    p_h = np.zeros((2, NCORES, 512), np.float32)
    s_h = np.zeros((2, NCORES * NSS, 512), np.float32)
    p_cv = np.zeros((2, NCORES, 3, 512), np.float32)
    s_cv = np.zeros((2, NCORES * NSS, 3, 512), np.float32)
    for c, r in enumerate(results):
        sl = slice(NSS * c, NSS * (c + 1))
        g = r["o_gla_p"].reshape(2, 2, 64, 2, 128)
        p_gla[:, c] = g.transpose(0, 3, 1, 2, 4).reshape(2, 4, 64, 128)
        g = r["o_gla_s"].reshape(2, 2, 64, NSS, 2, 128)
        s_gla[:, sl] = g.transpose(0, 3, 4, 1, 2, 5).reshape(2, NSS, 4, 64, 128)
        p_h[:, c] = r["o_h_p"].reshape(2, 128, 4).transpose(0, 2, 1).reshape(2, 512)
        s_h[:, sl] = r["o_h_s"].reshape(2, 128, 4, NSS).transpose(0, 3, 2, 1).reshape(2, NSS, 512)
        p_cv[:, c] = r["o_conv_p"].reshape(2, 128, 4, 3).transpose(0, 3, 2, 1).reshape(2, 3, 512)
        s_cv[:, sl] = r["o_conv_s"].reshape(2, 128, 4, NSS, 3).transpose(0, 3, 4, 2, 1).reshape(2, NSS, 3, 512)
    p_rw = np.zeros((2, NCORES, 16, 64, 64), np.float32)
    s_rw = np.zeros((2, NCORES * NSS, 16, 64, 64), np.float32)
    p_sh = np.zeros((2, NCORES, D), np.float32)
    s_sh = np.zeros((2, NCORES * NSS, D), np.float32)
    for c, r in enumerate(results):
        sl = slice(NSS * c, NSS * (c + 1))
        a = r["o_rw_p"].reshape(2, 2, 64, 8, 64)
        p_rw[:, c] = a.transpose(0, 3, 1, 4, 2).reshape(2, 16, 64, 64)
        a = r["o_rw_s"].reshape(2, 8, 2, 64, NSS, 64)
        s_rw[:, sl] = a.transpose(0, 4, 1, 2, 5, 3).reshape(2, NSS, 16, 64, 64)
        p_sh[:, c] = r["o_sh_p"].reshape(2, 128, 8).transpose(0, 2, 1).reshape(2, D)
        s_sh[:, sl] = r["o_sh_s"].reshape(2, 128, 8, NSS).transpose(0, 3, 2, 1).reshape(2, NSS, D)
    out.update(p_rw=p_rw, s_rw=s_rw, p_sh=p_sh, s_sh=s_sh)
    out.update(p_gla=p_gla, s_gla=s_gla, p_h=p_h, s_h=s_h, p_cv=p_cv, s_cv=s_cv)
    return out


def kernel(**inputs):
    prog = Prog()
    nc = prog.build()
    maps = make_in_maps(inputs)
    res = run_bass_kernel_spmd(nc, maps, core_ids=list(range(NCORES)))
    o = assemble(res.results)
    z = lambda *s: np.zeros(s, np.float32)
    y_p = np.stack([np.ascontiguousarray(r["o_yT"][:, NMETA:TP].T) for r in res.results])
    y_s = np.concatenate([np.ascontiguousarray(r["o_yT"][:, TP:].T).reshape(NSS, TS, D) for r in res.results])
    return (y_p, y_s, o["p_gla"], o["p_h"], o["p_cv"], o["p_rw"], o["p_sh"],
            o["s_gla"], o["s_h"], o["s_cv"], o["s_rw"], o["s_sh"])
```

```python
import numpy as np
import concourse.bass as bass
import concourse.mybir as mybir
from concourse.bass_utils import run_bass_kernel_spmd

F32 = mybir.dt.float32
BF16 = mybir.dt.bfloat16
U32 = mybir.dt.uint32
AF = mybir.ActivationFunctionType
ALU = mybir.AluOpType
AX = mybir.AxisListType

NCORES = 8
D = 1024
DEPTH = 4
NMETA = 16
TP = 2064
NSS = 16
TS = 4
DN_ALPHA = float((2 * DEPTH) ** 0.25)
LN_EPS = 1e-5
MIX_IN = 2576


class Tl:
    def __init__(self, ap):
        self.ap = ap
        self.buf = self
        self.bufs = [self]
        self.w = None
        self.r = []

    def __getitem__(self, k):
        return Vw(self.ap[k], self)

    def v(self, ap):
        return Vw(ap, self)

    def bitcast(self, dt):
        return Vw(self.ap.bitcast(dt), self)

    def re(self, pat, **kw):
        return Vw(self.ap.rearrange(pat, **kw), self)

    def bc(self, shape):
        return Vw(self.ap.to_broadcast(list(shape)), self)


class Vw:
    def __init__(self, ap, buf):
        self.ap = ap
        self.buf = buf
        self.bufs = buf if isinstance(buf, list) else [buf]

    def __getitem__(self, k):
        return Vw(self.ap[k], self.bufs)

    def bc(self, shape):
        return Vw(self.ap.to_broadcast(list(shape)), self.bufs)

    def re(self, pat, **kw):
        return Vw(self.ap.rearrange(pat, **kw), self.bufs)

    def bitcast(self, dt):
        return Vw(self.ap.bitcast(dt), self.bufs)


def _ap(x):
    return x.ap if isinstance(x, (Tl, Vw)) else x


class Eng:
    def __init__(self, nc, name, h):
        self.name = name
        self.h = h
        self.sem = nc.alloc_semaphore("s_" + name)
        self.cnt = 0
        self.known = {}
        self.dsems = []
        self.dvals = []
        self.dma_i = 0


class FW:
    NDS = 8

    def __init__(self, nc):
        self.nc = nc
        self.pe = Eng(nc, "pe", nc.tensor)
        self.act = Eng(nc, "act", nc.scalar)
        self.dve = Eng(nc, "dve", nc.vector)
        self.pool = Eng(nc, "pool", nc.gpsimd)
        self.sp = Eng(nc, "sp", nc.sync)
        for q in (self.sp, self.pool, self.act):
            q.dsems = [nc.alloc_semaphore("d_%s%d" % (q.name, i)) for i in range(self.NDS)]
            q.dvals = [0] * self.NDS
        self.out_events = []
        self.ninst = 0

    def sb(self, name, shape, dt=F32):
        return Tl(self.nc.alloc_sbuf_tensor(name, list(shape), dt).ap())

    def arena_init(self, nbytes, reg=8192):
        self.a_reg = reg
        self.a_big = self.nc.alloc_sbuf_tensor("arena", [128, nbytes // 2], BF16).ap()
        self.a_tl = [Tl(self.a_big[:, k * reg // 2:(k + 1) * reg // 2]) for k in range(nbytes // reg)]

    def av(self, off, shape, dt=F32):
        esz = 4 if dt in (F32, U32) else 2
        nel = int(np.prod(shape[1:]))
        nb = nel * esz
        assert off % 4 == 0 and off + nb <= len(self.a_tl) * self.a_reg, (off, nb)
        ap = self.a_big[:, off // 2:(off + nb) // 2]
        if esz == 4:
            ap = ap.bitcast(dt)
        if len(shape) > 2:
            names = " ".join("d%d" % i for i in range(len(shape) - 1))
            ap = ap.rearrange("p (%s) -> p %s" % (names, names), **{"d%d" % i: shape[i + 1] for i in range(len(shape) - 1)})
        bufs = self.a_tl[off // self.a_reg:(off + nb - 1) // self.a_reg + 1]
        return Vw(ap, list(bufs))

    def _wait(self, eng, ev):
        key, sem, val = ev
        if eng.known.get(key, 0) >= val:
            return
        eng.h.wait_ge(sem, val)
        self.ninst += 1
        eng.known[key] = val

    def _deps(self, eng, reads, writes):
        deps = []
        for v in reads:
            for b in v.bufs:
                if b.w is not None:
                    deps.append(b.w)
        for v in writes:
            for b in v.bufs:
                if b.w is not None:
                    deps.append(b.w)
                deps.extend(b.r)
        for ev in deps:
            if eng is self.pe and ev[0] == "pe":
                continue
            self._wait(eng, ev)

    def _record(self, ev, reads, writes):
        for v in reads:
            for b in v.bufs:
                b.r.append(ev)
        for v in writes:
            for b in v.bufs:
                b.w = ev
                b.r = []

    def op(self, eng, fn, reads=(), writes=()):
        reads = [b for b in reads if isinstance(b, (Tl, Vw))]
        writes = [b for b in writes if isinstance(b, (Tl, Vw))]
        self._deps(eng, reads, writes)
        ins = fn()
        eng.cnt += 1
        ins.then_inc(eng.sem, 1)
        self.ninst += 1
        self._record((eng.name, eng.sem, eng.cnt), reads, writes)

    def dma(self, q, out, in_, is_output=False, nc_ok=False):
        slot = q.dma_i % self.NDS
        q.dma_i += 1
        sem = q.dsems[slot]
        prev = q.dvals[slot]
        key = (q.name, "d", slot)
        if prev > 0:
            self._wait(q, (key, sem, prev))
        reads = [in_] if isinstance(in_, (Tl, Vw)) else []
        writes = [out] if isinstance(out, (Tl, Vw)) else []
        self._deps(q, reads, writes)
        kw = {}
        if nc_ok:
            kw["allow_slow_non_contiguous"] = True
        q.h.dma_start(out=_ap(out), in_=_ap(in_), **kw).then_inc(sem, 16)
        self.ninst += 1
        q.dvals[slot] = prev + 16
        ev = (key, sem, prev + 16)
        self._record(ev, reads, writes)
        if is_output:
            self.out_events.append(ev)

    def finish(self):
        for ev in self.out_events:
            self._wait(self.sp, ev)
        for e in (self.pe, self.act, self.dve, self.pool):
            if e.cnt:
                self._wait(self.sp, (e.name, e.sem, e.cnt))

    def mm(self, out, lhsT, rhs, start=True, stop=True):
        self.op(self.pe, lambda: self.nc.tensor.matmul(_ap(out), lhsT=_ap(lhsT), rhs=_ap(rhs), start=start, stop=stop),
                reads=[lhsT, rhs], writes=[out])

    def tr(self, out, in_, ident):
        self.op(self.pe, lambda: self.nc.tensor.transpose(_ap(out), _ap(in_), _ap(ident)),
                reads=[in_, ident], writes=[out])

    def actv(self, out, in_, func, bias=None, scale=1.0):
        kw = {}
        rd = [in_]
        if bias is not None:
            kw["bias"] = _ap(bias)
            rd.append(bias)
        if isinstance(scale, (Tl, Vw)):
            rd.append(scale)
        kw["scale"] = _ap(scale)
        self.op(self.act, lambda: self.nc.scalar.activation(out=_ap(out), in_=_ap(in_), func=func, **kw),
                reads=rd, writes=[out])

    def tt(self, out, a, b, op, eng=None):
        eng = eng or self.dve
        self.op(eng, lambda: eng.h.tensor_tensor(out=_ap(out), in0=_ap(a), in1=_ap(b), op=op),
                reads=[a, b], writes=[out])

    def ts(self, out, a, s1, s2, op0, op1=ALU.bypass, eng=None):
        eng = eng or self.dve
        self.op(eng, lambda: eng.h.tensor_scalar(out=_ap(out), in0=_ap(a), scalar1=_ap(s1), scalar2=_ap(s2), op0=op0, op1=op1),
                reads=[a, s1, s2], writes=[out])

    def stt(self, out, a, s, b, op0, op1):
        self.op(self.dve, lambda: self.nc.vector.scalar_tensor_tensor(out=_ap(out), in0=_ap(a), scalar=_ap(s), in1=_ap(b), op0=op0, op1=op1),
                reads=[a, s, b], writes=[out])

    def cp(self, out, in_, eng=None):
        eng = eng or self.dve
        if eng is self.act:
            self.op(eng, lambda: self.nc.scalar.copy(out=_ap(out), in_=_ap(in_)), reads=[in_], writes=[out])
        else:
            self.op(eng, lambda: eng.h.tensor_copy(out=_ap(out), in_=_ap(in_)), reads=[in_], writes=[out])

    def memset(self, out, val, eng=None):
        eng = eng or self.dve
        self.op(eng, lambda: eng.h.memset(_ap(out), val), writes=[out])

    def scan(self, out, d0, d1, init, op0=ALU.mult, op1=ALU.add):
        self.op(self.dve, lambda: self.nc.vector.tensor_tensor_scan(out=_ap(out), data0=_ap(d0), data1=_ap(d1), initial=_ap(init), op0=op0, op1=op1),
                reads=[d0, d1, init], writes=[out])


class Prog:
    def __init__(self, debug=None):
        self.debug = debug
        nc = self.nc = bass.Bass("TRN2", target_bir_lowering=False)
        self.fw = FW(nc)
        self.ins = {}
        self.outs = {}
        self.psn = 0
        self.nrot = 6

    def din(self, name, shape, dt=F32):
        t = self.nc.dram_tensor(name, list(shape), dt, kind="ExternalInput").ap()
        self.ins[name] = t
        return t

    def dout(self, name, shape, dt=F32):
        t = self.nc.dram_tensor(name, list(shape), dt, kind="ExternalOutput").ap()
        self.outs[name] = t
        return t

    def psum(self):
        t = self.ps[self.psn % self.nrot]
        self.psn += 1
        return t

    def build(self):
        fw = self.fw
        nc = self.nc
        sb = fw.sb
        self.o = {}
        xT_p = self.din("xT_p", [D, TP])
        xT_s = self.din("xT_s", [D, NSS * TS])
        cst = self.din("consts", [128, 2816])
        self.d_ev_w_in = self.din("ev_w_in", [2, D, MIX_IN])
        self.d_ev_w_gate = self.din("ev_w_gate", [2, 16, 256])
        self.d_ev_vecs = self.din("ev_vecs", [2, 128, 40])
        self.d_ev_lruw = self.din("ev_lruw", [2, 2, 4, 128, 128])
        self.d_ev_w_out = self.din("ev_w_out", [2, D, D])
        self.d_st_gla = self.din("st_gla", [2, 128, NSS * 2 * 128])
        self.d_st_h = self.din("st_h", [2, 128, 4 * NSS])
        self.d_st_conv = self.din("st_conv", [2, 128, 4 * NSS * 3])
        self.d_od_w = {nm: self.din("od_" + nm, [2, D, D]) for nm in ("w_r", "w_k", "w_v", "w_o")}
        self.d_od_l1 = self.din("od_l1", [2, D, 288])
        self.d_od_w2 = self.din("od_w2", [2, 64, D])
        self.d_od_a2 = self.din("od_a2", [2, 64, D])
        self.d_od_v2 = self.din("od_v2", [2, 32, D])
        self.d_od_g2 = self.din("od_g2", [2, 128, D])
        self.d_od_vecs = self.din("od_vecs", [2, 128, 112])
        self.d_st_rw = self.din("st_rw", [2, 8, 128, NSS * 64])
        self.d_st_sh = self.din("st_sh", [2, 128, 8 * NSS])
        self.o["rw_p"] = self.dout("o_rw_p", [2, 128, 8 * 64])
        self.o["rw_s"] = self.dout("o_rw_s", [2, 8, 128, NSS * 64])
        self.o["sh_p"] = self.dout("o_sh_p", [2, 128, 8])
        self.o["sh_s"] = self.dout("o_sh_s", [2, 128, 8 * NSS])
        self.d_lnp = self.din("lnp", [128, 128])
        self.d_wq = self.din("peer_wq", [DEPTH, D, 2048])
        self.d_keysT = self.din("peer_keysT", [DEPTH, 128, 2048])
        if not (self.debug or {}).get("nopeer"):
            self.d_uT = self.din("peer_uT", [DEPTH, D, 16384])
            self.d_v = self.din("peer_v", [DEPTH, 16384, D])
        self.o_yT = self.dout("o_yT", [D, TP + NSS * TS])
        o_gla_p = self.dout("o_gla_p", [2, 128, 2 * 128])
        o_gla_s = self.dout("o_gla_s", [2, 128, NSS * 2 * 128])
        o_h_p = self.dout("o_h_p", [2, 128, 4])
        o_h_s = self.dout("o_h_s", [2, 128, 4 * NSS])
        o_conv_p = self.dout("o_conv_p", [2, 128, 4 * 3])
        o_conv_s = self.dout("o_conv_s", [2, 128, 4 * NSS * 3])
        self.o.update(gla_p=o_gla_p, gla_s=o_gla_s, h_p=o_h_p, h_s=o_h_s, conv_p=o_conv_p, conv_s=o_conv_s)

        self.ps = [Tl(nc.alloc_psum_tensor("ps%d" % i, [128, 512], F32).ap()) for i in range(8)]
        self.cst = sb("cst", [128, 2816])
        fw.dma(fw.sp, self.cst, cst)
        self.ident = self.cst[:, 0:128]
        self.m_incl = self.cst[:, 128:256]
        self.m_strict = self.cst[:, 256:384]
        self.m_sincl = self.cst[:, 384:512]
        self.m_sstrict = self.cst[:, 512:640]
        self.ones = self.cst[:, 640:768]
        self.identb = sb("identb", [128, 128], BF16)
        fw.cp(self.identb, self.ident)
        self.onesb = sb("onesb", [128, 128], BF16)
        fw.cp(self.onesb, self.ones)

        self.x = sb("x", [128, 8, 128])
        self.xb = sb("xb", [128, 8, 128], BF16)

        self.gla_S = [sb("glaS%d" % j, [128, 2, 128]) for j in range(2)]
        self.lru_h = [sb("lruh%d" % j, [128, 4, 1]) for j in range(2)]
        self.lru_cv = [sb("lrucv%d" % j, [128, 4, 1, 3]) for j in range(2)]
        for j in range(2):
            fw.memset(self.gla_S[j], 0.0)
            fw.memset(self.lru_h[j], 0.0)
            fw.memset(self.lru_cv[j], 0.0)
        fw.arena_init(122880)
        A = self.A = {}
        self.w_in = fw.av(0, [128, 8, MIX_IN], BF16)
        self.w_out = fw.av(41216, [128, 8, D], BF16)
        A["Sb"] = fw.av(57600, [128, NSS, 4, 128], BF16)
        A["glaSs"] = fw.av(73984, [128, NSS, 2, 128], F32)
        A["wq"] = fw.av(65536, [128, 8, 2048], BF16)
        A["CT"] = fw.av(32768, [128, 128, 128], BF16)
        A["UT"] = [fw.av(65536 + 8192 * k, [128, 8, 512], BF16) for k in range(2)]
        A["V"] = [fw.av(81920 + 8192 * k, [128, 4, D], BF16) for k in range(2)]
        A["X"] = fw.av(98304, [128, 32, 128], BF16)
        A["J"] = fw.av(106496, [128, 32, 128], BF16)
        A["s_sb"] = fw.av(0, [128, 16, 128], F32)
        A["s2"] = fw.av(8192, [128, 16, 128], F32)
        A["qT"] = fw.av(16384, [128, 16, 128], F32)
        A["hm"] = fw.av(24576, [128, 8, 128], F32)
        A["hv"] = fw.av(28672, [128, 8, 128], F32)
        A["ht"] = fw.av(114688, [128, 8, 128], F32)
        A["ysb"] = fw.av(118784, [128, 1024], F32)
        A["hg"] = fw.av(114688, [128, 2, 512], BF16)
        R = self.R = {}
        R["w_r"] = fw.av(0, [128, 8, D], BF16)
        R["w_k"] = fw.av(16384, [128, 8, D], BF16)
        R["w_v"] = fw.av(32768, [128, 8, D], BF16)
        R["l1"] = fw.av(49152, [128, 8, 288], BF16)
        R["w2"] = fw.av(53760, [128, D], BF16)
        R["a2"] = fw.av(55808, [128, D], BF16)
        R["v2"] = fw.av(57856, [128, D], BF16)
        R["g2"] = fw.av(59904, [128, D], BF16)
        R["xi"] = [fw.av(61952 + 2048 * k, [128, 8, 128], BF16) for k in range(6)]
        R["xx"] = fw.av(74240, [128, 8, 128], F32)
        R["STs"] = fw.av(78336, [128, NSS, 64], F32)
        R["STb"] = fw.av(82432, [128, NSS, 64], BF16)
        R["Sz"] = fw.av(84480, [128, NSS, 2, 64], BF16)
        R["Apad"] = fw.av(88576, [128, NSS, 64], BF16)
        R["Rpad"] = fw.av(90624, [128, NSS, 64], BF16)
        R["Apad2"] = fw.av(112640, [128, NSS, 64], BF16)
        R["bGpad"] = fw.av(92672, [128, NSS, 128], BF16)
        R["kGpad"] = fw.av(96768, [128, NSS, 128], BF16)
        for k, nm in enumerate(("r32", "k32", "v32", "a32", "g32", "lw", "kk", "k2", "bv", "cum", "cp", "E", "E3", "t", "t2", "y32", "gate")):
            R[nm] = fw.av(100864 + 512 * k, [128, 128], F32)
        self.gla_Ss = [A["glaSs"], A["glaSs"]]
        self.rw_S = [sb("rwS%d" % j, [128, 8, 64]) for j in range(2)]
        self.sh_p = [sb("shp%d" % j, [128, 8, 1]) for j in range(2)]
        self.sh_s = [sb("shs%d" % j, [128, 8, NSS]) for j in range(2)]
        self.vfirst = sb("vfirst", [128, 8, 128])
        self.odv = sb("odv", [128, 112])
        for j in range(2):
            fw.memset(self.rw_S[j], 0.0)
            fw.memset(self.sh_p[j], 0.0)
            fw.dma(fw.sp, self.sh_s[j], self.d_st_sh[j].rearrange("p (c s) -> p c s", c=8))
        self.lru_hs = [sb("lruhs%d" % j, [128, 4, NSS]) for j in range(2)]
        self.lru_cvs = [sb("lrucvs%d" % j, [128, 4, NSS, 3]) for j in range(2)]
        for j in range(2):
            fw.dma(fw.sp, self.lru_hs[j], self.d_st_h[j].rearrange("p (c s) -> p c s", c=4))
            fw.dma(fw.sp, self.lru_cvs[j], self.d_st_conv[j].rearrange("p (c s i) -> p c s i", c=4, s=NSS))

        self.w_gate = sb("w_gate", [16, 256], BF16)
        self.lnp = sb("lnp_sb", [128, 128])
        fw.dma(fw.sp, self.lnp, self.d_lnp)
        self.keysT = sb("keysT_sb", [128, 2048])
        self.lruw = sb("lruw", [128, 2, 4, 128], BF16)
        self.evv = sb("evv", [128, 40])
        self.c8 = [sb("c8_%d" % j, [128, 4]) for j in range(2)]
        self.negevv = sb("negevv", [128, 40])
        self.negodv = sb("negodv", [128, 112])
        self.work_init()

        tiles = [("p", 128 * k, 128) for k in range(16)] + [("p", 2048, 16), ("s", 0, 64)]
        if self.debug:
            tiles = self.debug.get("tiles", tiles)
        layers = (self.debug or {}).get("layers", [0, 1, 2, 3])
        self.precast_all()
        for (kind, t0, n) in tiles:
            src = xT_p if kind == "p" else xT_s
            for c in range(8):
                fw.dma(fw.sp, self.x[:, c, 0:n], src[c * 128:(c + 1) * 128, t0:t0 + n])
            fw.cp(self.xb[:, :, 0:n], self.x[:, :, 0:n])
            for layer in layers:
                j = layer // 2
                if layer % 2 == 0:
                    self.load_even_weights(j)
                    if kind == "s":
                        fw.dma(fw.sp, self.gla_Ss[j], self.d_st_gla[j].rearrange("p (s h v) -> p s h v", s=NSS, h=2))
                    self.gla_lru_tile(j, kind, n)
                    if kind == "s":
                        fw.dma(fw.sp, self.o["gla_s"][j].rearrange("p (s h v) -> p s h v", s=NSS, h=2), self.gla_Ss[j], is_output=True)
                    self.out_proj(n, self.w_out)
                else:
                    self.rwkv_tile(j, kind, n)
                    self.out_proj(n, self.R["w_r"])
                self.ln_tile(n, layer, 0)
                if (self.debug or {}).get("nopeer"):
                    continue
                self.peer_tile(layer, n)
                self.ln_tile(n, layer, 1)
            o0 = t0 if kind == "p" else TP
            for c in range(8):
                fw.dma(fw.sp, self.o_yT[c * 128:(c + 1) * 128, o0:o0 + n], self.x[:, c, 0:n], is_output=True)
        for j in range(2):
            fw.dma(fw.sp, self.o["gla_p"][j].rearrange("p (h v) -> p h v", h=2), self.gla_S[j], is_output=True)
            fw.dma(fw.sp, self.o["rw_p"][j].rearrange("p (c v) -> p c v", c=8), self.rw_S[j], is_output=True)
            fw.dma(fw.sp, self.o["sh_p"][j].rearrange("p (c o) -> p c o", o=1), self.sh_p[j], is_output=True)
            fw.dma(fw.sp, self.o["sh_s"][j].rearrange("p (c s) -> p c s", c=8), self.sh_s[j], is_output=True)
            fw.dma(fw.sp, self.o["h_p"][j].rearrange("p (c o) -> p c o", o=1), self.lru_h[j], is_output=True)
            fw.dma(fw.sp, self.o["h_s"][j].rearrange("p (c s) -> p c s", c=4), self.lru_hs[j], is_output=True)
            fw.dma(fw.sp, self.o["conv_p"][j].rearrange("p (c o i) -> p c o i", c=4, o=1), self.lru_cv[j], is_output=True)
            fw.dma(fw.sp, self.o["conv_s"][j].rearrange("p (c s i) -> p c s i", c=4, s=NSS), self.lru_cvs[j], is_output=True)
        fw.finish()
        return nc

    def load_even_weights(self, j):
        fw = self.fw
        fw.dma(fw.sp, self.w_in, self.sc_["w_in%d" % j].re("(c p) n -> p c n", p=128))
        fw.dma(fw.sp, self.w_out, self.sc_["w_out%d" % j].re("(c p) n -> p c n", p=128))
        fw.dma(fw.pool, self.w_gate, self.d_ev_w_gate[j])
        fw.dma(fw.pool, self.lruw, self.d_ev_lruw[j].rearrange("g c i o -> i g c o"))
        fw.dma(fw.sp, self.evv, self.d_ev_vecs[j])
        fw.actv(self.c8[j], self.evv[:, 31:35], AF.Exp, scale=-1.0)
        fw.actv(self.c8[j], self.c8[j], AF.Ln, bias=self.cst[:, 640:641])
        fw.ts(self.c8[j], self.c8[j], -8.0, None, ALU.mult)
        fw.ts(self.negevv, self.evv, -1.0, None, ALU.mult)

    def out_proj(self, n, w_out):
        fw, W = self.fw, self.W
        for oc in range(8):
            ps = self.psum()
            for ic in range(8):
                fw.mm(ps[:, 0:n], w_out[:, ic, oc * 128:(oc + 1) * 128], W["mix"][:, ic, 0:n], start=(ic == 0), stop=(ic == 7))
            fw.stt(self.x[:, oc, 0:n], self.x[:, oc, 0:n], DN_ALPHA, ps[:, 0:n], ALU.mult, ALU.add)

    def ln_tile(self, n, layer, which):
        fw, W, x = self.fw, self.W, self.x
        fw.tt(W["xsq"][:, :, 0:n], x[:, :, 0:n], x[:, :, 0:n], ALU.mult)
        pm = self.psum()
        for c in range(8):
            fw.mm(pm[:, 0:n], self.ones, x[:, c, 0:n], start=(c == 0), stop=(c == 7))
        pq = self.psum()
        for c in range(8):
            fw.mm(pq[:, 0:n], self.ones, W["xsq"][:, c, 0:n], start=(c == 0), stop=(c == 7))
        mean, rstd, t1 = W["mean"], W["rstd"], W["t1"]
        fw.actv(mean[:, 0:n], pm[:, 0:n], AF.Copy, scale=1.0 / D)
        fw.actv(rstd[:, 0:n], pq[:, 0:n], AF.Copy, scale=1.0 / D)
        fw.tt(t1[:, 0:n], mean[:, 0:n], mean[:, 0:n], ALU.mult)
        fw.tt(rstd[:, 0:n], rstd[:, 0:n], t1[:, 0:n], ALU.subtract)
        fw.actv(rstd[:, 0:n], rstd[:, 0:n], AF.Ln, bias=self.cst[:, 896:897])
        fw.actv(rstd[:, 0:n], rstd[:, 0:n], AF.Exp, scale=-0.5)
        fw.tt(x[:, :, 0:n], x[:, :, 0:n], mean[:, 0:n].re("p (o n) -> p o n", o=1).bc([128, 8, n]), ALU.subtract)
        fw.tt(x[:, :, 0:n], x[:, :, 0:n], rstd[:, 0:n].re("p (o n) -> p o n", o=1).bc([128, 8, n]), ALU.mult)
        col = (layer * 2 + which) * 8
        for c in range(8):
            fw.ts(x[:, c, 0:n], x[:, c, 0:n], self.lnp[:, col + c:col + c + 1], self.lnp[:, 64 + col + c:64 + col + c + 1], ALU.mult, ALU.add)
        fw.cp(self.xb[:, :, 0:n], x[:, :, 0:n])

    def sigm(self, out, in_, negbias=None):
        fw = self.fw
        fw.actv(out, in_, AF.Exp, bias=negbias, scale=-1.0)
        fw.ts(out, out, 1.0, None, ALU.add)
        self.vop("reciprocal", [out], [out], out=out, in_=out)

    def vop(self, name, reads, writes, **kw):
        fw = self.fw
        fw.op(fw.dve, lambda: getattr(self.nc.vector, name)(**{k: _ap(v) for k, v in kw.items()}), reads=reads, writes=writes)

    def precast_all(self):
        fw, nc = self.fw, self.nc
        self.sc_ = {}

        def pc(key, src, rows, cols, step):
            t = Tl(nc.dram_tensor("sc_" + key, [rows, cols], BF16, kind="Internal").ap())
            for r0 in range(0, rows, step):
                fw.dma(fw.pool, t[r0:r0 + step, :], src[r0:r0 + step, :])
            self.sc_[key] = t
        self.uTb, self.vbb = [None] * DEPTH, [None] * DEPTH
        for layer in range(DEPTH):
            j = layer // 2
            if layer % 2 == 0:
                pc("w_in%d" % j, self.d_ev_w_in[j], D, MIX_IN, 256)
                pc("w_out%d" % j, self.d_ev_w_out[j], D, D, 512)
            else:
                for nm in ("w_r", "w_k", "w_v", "w_o"):
                    pc("%s%d" % (nm, j), self.d_od_w[nm][j], D, D, 512)
            pc("wq%d" % layer, self.d_wq[layer], D, 2048, 256)
            if not (self.debug or {}).get("nopeer"):
                pc("uT%d" % layer, self.d_uT[layer], D, 16384, 128)
                pc("v%d" % layer, self.d_v[layer], 16384, D, 2048)
                self.uTb[layer], self.vbb[layer] = self.sc_["uT%d" % layer], self.sc_["v%d" % layer]

    def peer_precast(self):
        fw, nc = self.fw, self.nc
        self.uTb = [Tl(nc.dram_tensor("uTb%d" % l, [D, 16384], BF16, kind="Internal").ap()) for l in range(DEPTH)]
        self.vbb = [Tl(nc.dram_tensor("vbb%d" % l, [16384, D], BF16, kind="Internal").ap()) for l in range(DEPTH)]
        for l in range(DEPTH):
            for k in range(8):
                fw.dma(fw.pool, self.uTb[l][k * 128:(k + 1) * 128, :], self.d_uT[l, k * 128:(k + 1) * 128, :])
            for k in range(8):
                fw.dma(fw.pool, self.vbb[l][k * 2048:(k + 1) * 2048, :], self.d_v[l, k * 2048:(k + 1) * 2048, :])

    def peer_load_u(self, layer, g):
        src = self.uTb[layer].re("(dc p) e -> p dc e", p=128)[:, :, g * 512:(g + 1) * 512]
        self.fw.dma(self.fw.sp, self.A["UT"][g % 2], src)

    def peer_load_v(self, layer, g):
        src = self.vbb[layer][g * 512:(g + 1) * 512, :].re("(c p) d -> p c d", p=128)
        self.fw.dma(self.fw.sp, self.A["V"][g % 2], src)

    def peer_tile(self, layer, n):
        fw, W, A, xb = self.fw, self.W, self.A, self.xb
        wq, qT, s_sb, s2, CT = A["wq"], A["qT"], A["s_sb"], A["s2"], A["CT"]
        fw.dma(fw.sp, wq, self.sc_["wq%d" % layer].re("(c p) n -> p c n", p=128))
        fw.dma(fw.sp, self.keysT, self.d_keysT[layer])
        for ch in range(16):
            ps = self.psum()
            for c in range(8):
                fw.mm(ps[:, 0:n], wq[:, c, ch * 128:(ch + 1) * 128], xb[:, c, 0:n], start=(c == 0), stop=(c == 7))
            fw.cp(qT[:, ch, 0:n], ps[:, 0:n], eng=fw.act)
        for g_ in range(2):
            self.peer_load_u(layer, g_)
            self.peer_load_v(layer, g_)
        fw.tt(s2[:, :, 0:n], qT[:, :, 0:n], qT[:, :, 0:n], ALU.mult)
        hm, hv, ht = W["hm"], W["hv"], W["ht"]
        for h in range(8):
            pm = self.psum()
            fw.mm(pm[:, 0:n], self.ones, qT[:, 2 * h, 0:n], start=True, stop=False)
            fw.mm(pm[:, 0:n], self.ones, qT[:, 2 * h + 1, 0:n], start=False, stop=True)
            fw.actv(hm[:, h, 0:n], pm[:, 0:n], AF.Copy, scale=1.0 / 256)
            pq = self.psum()
            fw.mm(pq[:, 0:n], self.ones, s2[:, 2 * h, 0:n], start=True, stop=False)
            fw.mm(pq[:, 0:n], self.ones, s2[:, 2 * h + 1, 0:n], start=False, stop=True)
            fw.actv(hv[:, h, 0:n], pq[:, 0:n], AF.Copy, scale=1.0 / 256)
        fw.tt(ht[:, :, 0:n], hm[:, :, 0:n], hm[:, :, 0:n], ALU.mult)
        fw.tt(hv[:, :, 0:n], hv[:, :, 0:n], ht[:, :, 0:n], ALU.subtract)
        fw.actv(hv[:, :, 0:n], hv[:, :, 0:n], AF.Ln, bias=self.cst[:, 896:897])
        fw.actv(hv[:, :, 0:n], hv[:, :, 0:n], AF.Exp, scale=-0.5)
        for half in range(2):
            qv = qT.re("p (h two) n -> p h two n", two=2)[:, :, half, 0:n]
            fw.tt(qv, qv, hm[:, :, 0:n], ALU.subtract)
            fw.tt(qv, qv, hv[:, :, 0:n], ALU.mult)
        for g in range(4):
            ps = self.psum()
            for q in range(4):
                ch = 4 * g + q
                fw.mm(ps[0:n, q * 128:(q + 1) * 128], qT[:, ch, 0:n], self.keysT[:, ch * 128:(ch + 1) * 128])
            fw.cp(s_sb[0:n, 4 * g:4 * g + 4, :], ps[0:n, :].re("p (a k) -> p a k", a=4), eng=fw.act)
        vv, idx = W["vv"], W["idx"]
        for l in range(16):
            sl, s2l = s_sb[0:n, l, :], s2[0:n, l, :]
            self.vop("max", [sl], [vv[0:n, l, 0:8]], out=vv[0:n, l, 0:8], in_=sl)
            self.vop("match_replace", [sl, vv[0:n, l, 0:8]], [s2l], out=s2l, in_to_replace=vv[0:n, l, 0:8], in_values=sl, imm_value=-1e30)
            self.vop("max", [s2l], [vv[0:n, l, 8:16]], out=vv[0:n, l, 8:16], in_=s2l)
            self.vop("max_index", [sl, vv[0:n, l, 0:8]], [idx[0:n, l, 0:8]], out=idx[0:n, l, 0:8], in_max=vv[0:n, l, 0:8], in_values=sl)
            self.vop("max_index", [sl, vv[0:n, l, 8:16]], [idx[0:n, l, 8:16]], out=idx[0:n, l, 8:16], in_max=vv[0:n, l, 8:16], in_values=sl)
        fw.cp(W["idxf"][0:n], idx[0:n])
        cand = s2.re("p a k -> p (a k)").re("p (h c) -> p h c", h=8)
        cand2 = qT.re("p a k -> p (a k)").re("p (h c) -> p h c", h=8)
        vv4 = vv.re("p (h two) k -> p h two k", two=2)
        if4 = W["idxf"].re("p (h two) k -> p h two k", two=2)
        c4 = cand.re("p h (a b) -> p h a b", a=16)
        fw.tt(c4[0:n], vv4[0:n, :, 0, :].re("p h (a o) -> p h a o", o=1).bc([n, 8, 16, 16]),
              vv4[0:n, :, 1, :].re("p h (o b) -> p h o b", o=1).bc([n, 8, 16, 16]), ALU.add)
        sc, pos = W["sc"], W["pos"]
        for h in range(8):
            ch_, c2 = cand[0:n, h, :], cand2[0:n, h, :]
            self.vop("max", [ch_], [sc[0:n, h, 0:8]], out=sc[0:n, h, 0:8], in_=ch_)
            self.vop("match_replace", [ch_, sc[0:n, h, 0:8]], [c2], out=c2, in_to_replace=sc[0:n, h, 0:8], in_values=ch_, imm_value=-1e30)
            self.vop("max", [c2], [sc[0:n, h, 8:16]], out=sc[0:n, h, 8:16], in_=c2)
            self.vop("max_index", [ch_, sc[0:n, h, 0:8]], [pos[0:n, h, 0:8]], out=pos[0:n, h, 0:8], in_max=sc[0:n, h, 0:8], in_values=ch_)
            self.vop("max_index", [ch_, sc[0:n, h, 8:16]], [pos[0:n, h, 8:16]], out=pos[0:n, h, 8:16], in_max=sc[0:n, h, 8:16], in_values=ch_)
        fw.ts(W["pa"][0:n], pos[0:n], 4, None, ALU.logical_shift_right)
        fw.cp(W["paf"][0:n], W["pa"][0:n])
        fw.ts(W["pa"][0:n], pos[0:n], 15, None, ALU.bitwise_and)
        fw.cp(W["pbf"][0:n], W["pa"][0:n])
        eq = cand2.re("p h (a b) -> p h a b", a=16)
        iota16 = self.cst[0:n, 2048:2064].re("p (x y a) -> p x y a", x=1, y=1).bc([n, 8, 16, 16])
        for (pf, half, dst) in ((W["paf"], 0, W["pe1"]), (W["pbf"], 1, W["pe2"])):
            fw.tt(eq[0:n], pf[0:n].re("p h (k o) -> p h k o", o=1).bc([n, 8, 16, 16]), iota16, ALU.is_equal)
            fw.tt(eq[0:n], eq[0:n], if4[0:n, :, half, :].re("p h (o a) -> p h o a", o=1).bc([n, 8, 16, 16]), ALU.mult)
            self.vop("tensor_reduce", [eq[0:n]], [dst[0:n]], out=dst[0:n], in_=eq[0:n], axis=AX.X, op=ALU.add)
        pg, pz = W["pg"], W["pz"]
        fw.tt(pg[0:n], sc[0:n], sc[0:n, :, 0:1].bc([n, 8, 16]), ALU.subtract)
        fw.actv(pg[0:n], pg[0:n], AF.Exp)
        self.vop("tensor_reduce", [pg[0:n]], [pz[0:n]], out=pz[0:n], in_=pg[0:n], axis=AX.X, op=ALU.add)
        self.vop("reciprocal", [pz[0:n]], [pz[0:n]], out=pz[0:n], in_=pz[0:n])
        fw.tt(pg[0:n], pg[0:n], pz[0:n].re("p (h o) -> p h o", o=1).bc([n, 8, 16]), ALU.mult)
        for (src, dst) in ((W["pe1"], W["e1T"]), (W["pe2"], W["e2T"]), (pg, W["gT"])):
            ps = self.psum()
            fw.tr(ps[:, 0:n], src[0:n].re("p h k -> p (h k)"), self.ident[0:n, 0:n])
            fw.cp(dst[:, 0:n], ps[:, 0:n])
        X, J = A["X"], A["J"]
        iota128 = self.cst[:, 2048:2176].re("p (o i) -> p o i", o=1)
        ev = 0
        for t0 in range(0, n, 32):
            nb = min(32, n - t0)
            bc3 = lambda v: v[:, t0:t0 + nb].re("p (t o) -> p t o", o=1).bc([128, nb, 128])
            fw.tt(X[:, 0:nb, :], iota128.bc([128, nb, 128]), bc3(W["e1T"]), ALU.is_equal)
            fw.tt(X[:, 0:nb, :], X[:, 0:nb, :], bc3(W["gT"]), ALU.mult)
            fw.tt(J[:, 0:nb, :], iota128.bc([128, nb, 128]), bc3(W["e2T"]), ALU.is_equal)
            for q0 in range(0, nb, 4):
                k = min(4, nb - q0)
                ps = self.psum()
                for q in range(k):
                    fw.mm(ps[:, q * 128:(q + 1) * 128], J[:, q0 + q, :], X[:, q0 + q, :])
                fw.cp(CT[:, t0 + q0:t0 + q0 + k, :], ps[:, 0:k * 128].re("p (t i) -> p t i", i=128), eng=(fw.act if ev % 2 else fw.dve))
                ev += 1
        y0, y1 = self.ps[6], self.ps[7]

        def h_group(g):
            ub = A["UT"][g % 2]
            ph = self.psum()
            for c in range(8):
                fw.mm(ph[0:n, :], xb[:, c, 0:n], ub[:, c, :], start=(c == 0), stop=(c == 7))
            hg = A["hg"][:, g % 2, :]
            fw.actv(hg[0:n, :], ph[0:n, :], AF.Gelu_apprx_tanh)
            if g + 2 < 32:
                self.peer_load_u(layer, g + 2)
            for q in range(4):
                i = 4 * g + q
                ptb = self.psum().bitcast(BF16)
                fw.tr(ptb[:, 0:n], hg[0:n, q * 128:(q + 1) * 128], self.identb[0:n, 0:n])
                fw.tt(W["pT"][i % 8][:, 0:n], ptb[:, 0:n], CT[:, 0:n, i], ALU.mult)

        def y_group(g):
            vb = A["V"][g % 2]
            for q in range(4):
                i = 4 * g + q
                pT = W["pT"][i % 8]
                fw.mm(y0[0:n, :], pT[:, 0:n], vb[:, q, 0:512], start=(i == 0), stop=(i == 127))
                fw.mm(y1[0:n, :], pT[:, 0:n], vb[:, q, 512:1024], start=(i == 0), stop=(i == 127))
            if g + 2 < 32:
                self.peer_load_v(layer, g + 2)
        h_group(0)
        for g in range(32):
            if g + 1 < 32:
                h_group(g + 1)
            y_group(g)
        ysb = W["ysb"]
        fw.cp(ysb[0:n, 0:512], y0[0:n, :])
        fw.cp(ysb[0:n, 512:1024], y1[0:n, :], eng=fw.act)
        for c in range(8):
            ps = self.psum()
            fw.tr(ps[:, 0:n], ysb[0:n, c * 128:(c + 1) * 128], self.ident[0:n, 0:n])
            fw.stt(self.x[:, c, 0:n], self.x[:, c, 0:n], DN_ALPHA, ps[:, 0:n], ALU.mult, ALU.add)

    def rwkv_tile(self, j, kind, n):
        fw, W, R, x = self.fw, self.W, self.R, self.x
        nseg, L = (1, n) if kind == "p" else (NSS, TS)
        nlev = {128: 6, 16: 3, 4: 1}[L]
        self.nrot = 5
        odv = self.odv
        cst = self.cst
        for nm in ("w_r", "w_k", "w_v"):
            fw.dma(fw.sp, R[nm], self.sc_["%s%d" % (nm, j)].re("(c p) n -> p c n", p=128))
        for c in range(8):
            fw.dma(fw.pool, R["l1"][:, c, :], self.d_od_l1[j, c * 128:(c + 1) * 128, :])
        fw.dma(fw.pool, R["w2"][0:64, :], self.d_od_w2[j])
        fw.dma(fw.pool, R["a2"][0:64, :], self.d_od_a2[j])
        fw.dma(fw.pool, R["v2"][0:32, :], self.d_od_v2[j])
        fw.dma(fw.pool, R["g2"], self.d_od_g2[j])
        fw.dma(fw.sp, odv, self.d_od_vecs[j])
        fw.ts(self.negodv, odv, -1.0, None, ALU.mult)
        sh = self.sh_p[j] if kind == "p" else self.sh_s[j]
        xv_ = x[:, :, 0:n].re("p c (s l) -> p c s l", s=nseg)
        xxv = R["xx"][:, :, 0:n].re("p c (s l) -> p c s l", s=nseg)
        if L > 1:
            fw.tt(xxv[:, :, :, 1:L], xv_[:, :, :, 0:L - 1], xv_[:, :, :, 1:L], ALU.subtract)
        fw.tt(xxv[:, :, :, 0:1], sh.re("p c (s o) -> p c s o", o=1), xv_[:, :, :, 0:1], ALU.subtract)
        fw.cp(sh.re("p c (s o) -> p c s o", o=1), xv_[:, :, :, L - 1:L])
        for i in range(6):
            for c in range(8):
                fw.stt(R["xi"][i][:, c, 0:n], R["xx"][:, c, 0:n], odv[:, i * 8 + c:i * 8 + c + 1], x[:, c, 0:n], ALU.mult, ALU.add)
        xr, xw, xk, xvv, xa, xg = R["xi"]
        l1 = R["l1"]

        def lora1(dst, src, c0, ncol, func):
            ps = self.psum()
            for c in range(8):
                fw.mm(ps[0:ncol, 0:n], l1[:, c, c0:c0 + ncol], src[:, c, 0:n], start=(c == 0), stop=(c == 7))
            fw.actv(dst[0:ncol, 0:n], ps[0:ncol, 0:n], func)
        lora1(W["w1x"], xw, 0, 64, AF.Tanh)
        lora1(W["a1x"], xa, 64, 64, AF.Copy)
        lora1(W["g1x"], xg, 128, 128, AF.Sigmoid)
        if j > 0:
            lora1(W["v1x"], xvv, 256, 32, AF.Copy)

        def pj(w, src, c):
            ps = self.psum()
            for dc in range(8):
                fw.mm(ps[:, 0:n], w[:, dc, c * 128:(c + 1) * 128], src[:, dc, 0:n], start=(dc == 0), stop=(dc == 7))
            return ps
        N_ = slice(0, n)
        rmask = cst[:, 768:896] if kind == "s" else self.ones
        m_strict = (self.m_sstrict if kind == "s" else self.m_strict)[0:n, 0:n]
        m_incl = (self.m_sincl if kind == "s" else self.m_incl)[0:n, 0:n]
        m_low = (cst[:, 2304:2432] if kind == "s" else cst[:, 2176:2304])[0:n, 0:n]
        segmask = cst[:, 1024:2048].re("p (s t) -> p s t", s=NSS)
        segoh = cst[0:n, 912:928].re("p (s o) -> p s o", o=1)
        r32, k32, v32, a32, g32, lw, kk, k2, bv = (R[k_] for k_ in ("r32", "k32", "v32", "a32", "g32", "lw", "kk", "k2", "bv"))
        cum, cpv, E, E3, t, t2, y32, gate = (R[k_] for k_ in ("cum", "cp", "E", "E3", "t", "t2", "y32", "gate"))
        for c in range(8):
            col = lambda base: odv[:, base + c:base + c + 1]
            fw.cp(r32[:, N_], pj(R["w_r"], xr, c)[:, N_], eng=fw.act)
            fw.cp(k32[:, N_], pj(R["w_k"], xk, c)[:, N_], eng=fw.act)
            fw.cp(v32[:, N_], pj(R["w_v"], xvv, c)[:, N_], eng=fw.act)
            if j == 0:
                fw.cp(self.vfirst[:, c, N_], v32[:, N_])
            else:
                ps = self.psum()
                fw.mm(ps[:, N_], R["v2"][0:32, c * 128:(c + 1) * 128], W["v1x"][0:32, N_])
                self.sigm(gate[:, N_], ps[:, N_], self.negodv[:, 104 + c:105 + c])
                fw.tt(t[:, N_], self.vfirst[:, c, N_], v32[:, N_], ALU.subtract)
                fw.tt(t[:, N_], t[:, N_], gate[:, N_], ALU.mult)
                fw.tt(v32[:, N_], v32[:, N_], t[:, N_], ALU.add)
            fw.cp(W["vbf"][:, N_], v32[:, N_])
            ps = self.psum()
            psb = ps.bitcast(BF16)
            fw.tr(psb[0:n, 0:128], W["vbf"][:, N_], self.identb)
            fw.cp(W["Vtm"][0:n, :], psb[0:n, 0:128])
            ps = self.psum()
            fw.mm(ps[:, N_], R["w2"][0:64, c * 128:(c + 1) * 128], W["w1x"][0:64, N_])
            fw.ts(t[:, N_], ps[:, N_], col(48), -1.0, ALU.add, ALU.mult)
            fw.actv(t[:, N_], t[:, N_], AF.Exp)
            fw.actv(t[:, N_], t[:, N_], AF.Ln, bias=cst[:, 640:641])
            fw.actv(t[:, N_], t[:, N_], AF.Exp, bias=cst[:, 897:898], scale=-1.0)
            fw.ts(lw[:, N_], t[:, N_], -1.0, None, ALU.mult)
            ps = self.psum()
            fw.mm(ps[:, N_], R["a2"][0:64, c * 128:(c + 1) * 128], W["a1x"][0:64, N_])
            self.sigm(a32[:, N_], ps[:, N_], self.negodv[:, 56 + c:57 + c])
            ps = self.psum()
            fw.mm(ps[:, N_], R["g2"][:, c * 128:(c + 1) * 128], W["g1x"][:, N_])
            fw.cp(g32[:, N_], ps[:, N_], eng=fw.act)
            fw.ts(kk[:, N_], k32[:, N_], col(64), None, ALU.mult)
            fw.tt(t[:, N_], kk[:, N_], kk[:, N_], ALU.mult)
            ps = self.psum()
            fw.mm(ps[:, N_], cst[:, 2432:2560], t[:, N_])
            fw.actv(t[:, N_], ps[:, N_], AF.Ln, bias=cst[:, 896:897])
            fw.actv(t[:, N_], t[:, N_], AF.Exp, scale=-0.5)
            fw.tt(kk[:, N_], kk[:, N_], t[:, N_], ALU.mult)
            fw.ts(t[:, N_], a32[:, N_], -1.0, col(72), ALU.add, ALU.mult)
            fw.stt(k2[:, N_], t[:, N_], 1.0, k32[:, N_], ALU.add, ALU.mult)
            fw.tt(bv[:, N_], kk[:, N_], a32[:, N_], ALU.mult)
            fw.scan(cum[:, N_], rmask[:, N_], lw[:, N_], 0.0)
            fw.tt(cpv[:, N_], cum[:, N_], lw[:, N_], ALU.subtract)
            cumv = cum[:, N_].re("p (s l) -> p s l", s=nseg)
            cumC = cumv[:, :, L - 1:L]
            AR = W["AR"]
            fw.actv(E[:, N_], cpv[:, N_], AF.Exp)
            fw.stt(AR[:, 0, N_], kk[:, N_], -1.0, E[:, N_], ALU.mult, ALU.mult)
            fw.actv(E[:, N_], cum[:, N_], AF.Exp)
            fw.tt(AR[:, 1, N_], r32[:, N_], E[:, N_], ALU.mult)
            fw.actv(E[:, N_], cum[:, N_], AF.Exp, scale=-1.0)
            fw.tt(W["Bt"][:, N_], bv[:, N_], E[:, N_], ALU.mult)
            fw.tt(W["Kt"][:, N_], k2[:, N_], E[:, N_], ALU.mult)
            fw.tt(E3[:, N_].re("p (s l) -> p s l", s=nseg), cumC.bc([128, nseg, L]), cumv, ALU.subtract)
            fw.actv(E3[:, N_], E3[:, N_], AF.Exp)
            fw.tt(W["bG"][:, N_], bv[:, N_], E3[:, N_], ALU.mult)
            fw.tt(W["kG"][:, N_], k2[:, N_], E3[:, N_], ALU.mult)
            fw.actv(W["gam"][:, 0:nseg], cumC.re("p s o -> p (s o)"), AF.Exp)
            for hp in range(2):
                hm_ = cst[:, 900 + hp:901 + hp]
                fw.ts(W["Az"][:, hp, N_], AR[:, 0, N_], hm_, None, ALU.mult)
                fw.ts(W["Bz"][:, hp, N_], W["Bt"][:, N_], hm_, None, ALU.mult)
                fw.ts(W["Kz"][:, hp, N_], W["Kt"][:, N_], hm_, None, ALU.mult)
            for (src, dst) in ((W["bG"], W["bGtm"]), (W["kG"], W["kGtm"])):
                ps = self.psum()
                psb = ps.bitcast(BF16)
                fw.tr(psb[0:n, 0:128], src[:, N_], self.identb)
                fw.cp(dst[0:n, :], psb[0:n, 0:128])
            if kind == "p":
                ST = self.rw_S[j][:, c, :].re("p (s v) -> p s v", s=1)
            else:
                ST = R["STs"]
                fw.dma(fw.sp, ST, self.d_st_rw[j, c].rearrange("p (s v) -> p s v", s=NSS))
            STb, Sz = R["STb"], R["Sz"]
            fw.cp(STb[:, 0:nseg, :], ST)
            for hp in range(2):
                fw.ts(Sz[:, 0:nseg, hp, :], ST, cst[:, 900 + hp:901 + hp], None, ALU.mult)
            psY = self.ps[6]
            psS = [self.ps[7], self.ps[5]]
            if kind == "s":
                fw.tt(R["Rpad"][:, :, N_], AR[:, 1:2, N_].bc([128, NSS, n]), segmask, ALU.mult)
                fw.tt(R["bGpad"][0:n], W["bGtm"][0:n, :].re("p (o k) -> p o k", o=1).bc([n, NSS, 128]), segoh.bc([n, NSS, 128]), ALU.mult)
                fw.tt(R["kGpad"][0:n], W["kGtm"][0:n, :].re("p (o k) -> p o k", o=1).bc([n, NSS, 128]), segoh.bc([n, NSS, 128]), ALU.mult)

            def chain(hp):
                hc = slice(hp * 64, hp * 64 + 64)
                H = W["hd"][hp]
                banks = (self.ps[0], self.ps[1]) if hp == 0 else (self.ps[2], self.ps[3])
                cnt = [0]

                def pb_():
                    cnt[0] += 1
                    return banks[cnt[0] % 2]
                Az, Bz, Kz = W["Az"][:, hp, N_], W["Bz"][:, hp, N_], W["Kz"][:, hp, N_]
                ps1 = pb_()
                fw.mm(ps1[0:n, 0:2 * n].re("p (w t) -> p w t", w=2), Bz, AR[:, :, N_])
                fw.tt(H["MT"][0:n, N_], ps1[0:n, 0:n], m_strict, ALU.mult)
                fw.tt(H["MrT"][0:n, N_], ps1[0:n, n:2 * n], m_incl, ALU.mult)
                yield
                ps2 = pb_()
                fw.mm(ps2[0:n, 0:2 * n].re("p (w t) -> p w t", w=2), Kz, AR[:, :, N_])
                fw.tt(H["NT"][0:n, N_], ps2[0:n, 0:n], m_strict, ALU.mult)
                fw.tt(H["NrT"][0:n, N_], ps2[0:n, n:2 * n], m_incl, ALU.mult)
                yield
                ps3 = pb_()
                fw.mm(ps3[0:n, N_], Az, W["Bt"][:, N_])
                P, PT, Pn, PTn, TT = H["Pa"], H["PTa"], H["Pb"], H["PTb"], H["TT"]
                fw.tt(P[0:n, N_], ps3[0:n, N_], m_low, ALU.mult)
                fw.cp(PT[0:n, N_], H["MT"][0:n, N_], eng=fw.act)
                fw.tt(TT[0:n, N_], H["MT"][0:n, N_], self.identb[0:n, 0:n], ALU.add)
                yield
                for lev in range(nlev):
                    pa = pb_()
                    fw.mm(pa[0:n, N_], PT[0:n, N_], P[0:n, N_])
                    pb = pb_()
                    fw.mm(pb[0:n, N_], P[0:n, N_], PT[0:n, N_])
                    fw.cp(Pn[0:n, N_], pa[0:n, N_])
                    fw.cp(PTn[0:n, N_], pb[0:n, N_], eng=fw.act)
                    yield
                    pt = pb_()
                    fw.mm(pt[0:n, N_], Pn[0:n, N_], TT[0:n, N_])
                    fw.tt(TT[0:n, N_], TT[0:n, N_], pt[0:n, N_], ALU.add)
                    P, PT, Pn, PTn = Pn, PTn, P, PT
                    yield
                psW = pb_()
                if kind == "p":
                    fw.mm(psW[0:n, 0:64], Az, STb[:, 0, :], start=True, stop=False)
                else:
                    Apad = R["Apad"] if hp == 0 else R["Apad2"]
                    fw.tt(Apad[:, :, N_], W["Az"][:, hp:hp + 1, N_].bc([128, NSS, n]), segmask, ALU.mult)
                    for s_ in range(NSS):
                        fw.mm(psW[0:n, 0:64], Apad[:, s_, N_], STb[:, s_, :], start=(s_ == 0), stop=False)
                fw.mm(psW[0:n, 0:64], H["NT"][0:n, N_], W["Vtm"][0:n, hc], start=False, stop=True)
                fw.cp(H["Wsb"][0:n, :], psW[0:n, 0:64])
                yield
                psZ = pb_()
                fw.mm(psZ[0:n, 0:64], TT[0:n, N_], H["Wsb"][0:n, :])
                fw.cp(H["Zsb"][0:n, :], psZ[0:n, 0:64], eng=fw.act)
                yield
                if kind == "p":
                    fw.mm(psY[hc, N_], Sz[:, 0, hp, :], AR[:, 1, N_], start=True, stop=False)
                else:
                    for s_ in range(NSS):
                        fw.mm(psY[hc, N_], Sz[:, s_, hp, :], R["Rpad"][:, s_, N_], start=(s_ == 0), stop=False)
                fw.mm(psY[hc, N_], H["Zsb"][0:n, :], H["MrT"][0:n, N_], start=False, stop=False)
                fw.mm(psY[hc, N_], W["Vtm"][0:n, hc], H["NrT"][0:n, N_], start=False, stop=True)
                if kind == "p":
                    fw.mm(psS[0][hc, 0:64], W["bGtm"][0:n, hc], H["Zsb"][0:n, :], start=True, stop=False)
                    fw.mm(psS[0][hc, 0:64], W["kGtm"][0:n, hc], W["Vtm"][0:n, hc], start=False, stop=True)
                else:
                    for s_ in range(NSS):
                        o_ = psS[s_ // 8][hc, (s_ % 8) * 64:(s_ % 8 + 1) * 64]
                        fw.mm(o_, R["bGpad"][0:n, s_, hc], H["Zsb"][0:n, :], start=True, stop=False)
                        fw.mm(o_, R["kGpad"][0:n, s_, hc], W["Vtm"][0:n, hc], start=False, stop=True)
            gens = [chain(0), chain(1)]
            while gens:
                for g_ in list(gens):
                    try:
                        next(g_)
                    except StopIteration:
                        gens.remove(g_)
            if kind == "p":
                fw.stt(ST[:, 0, :], ST[:, 0, :], W["gam"][:, 0:1], psS[0][:, 0:64], ALU.mult, ALU.add)
            else:
                fw.tt(ST, ST, W["gam"][:, 0:NSS].re("p (s o) -> p s o", o=1).bc([128, NSS, 64]), ALU.mult)
                for g_ in range(2):
                    fw.tt(ST[:, 8 * g_:8 * g_ + 8, :], ST[:, 8 * g_:8 * g_ + 8, :], psS[g_][:, :].re("p (s v) -> p s v", s=8), ALU.add)
                fw.dma(fw.sp, self.o["rw_s"][j, c].rearrange("p (s v) -> p s v", s=NSS), ST, is_output=True)
            fw.cp(y32[:, N_], psY[:, N_])
            pm = self.psum()
            fw.mm(pm[:, N_], cst[:, 2560:2688], y32[:, N_])
            fw.tt(t[:, N_], y32[:, N_], y32[:, N_], ALU.mult)
            pq = self.psum()
            fw.mm(pq[:, N_], cst[:, 2560:2688], t[:, N_])
            fw.cp(t2[:, N_], pm[:, N_], eng=fw.act)
            fw.tt(y32[:, N_], y32[:, N_], t2[:, N_], ALU.subtract)
            fw.tt(t2[:, N_], t2[:, N_], t2[:, N_], ALU.mult)
            fw.tt(t2[:, N_], pq[:, N_], t2[:, N_], ALU.subtract)
            fw.actv(t2[:, N_], t2[:, N_], AF.Ln, bias=cst[:, 898:899])
            fw.actv(t2[:, N_], t2[:, N_], AF.Exp, scale=-0.5)
            fw.tt(y32[:, N_], y32[:, N_], t2[:, N_], ALU.mult)
            fw.ts(y32[:, N_], y32[:, N_], col(88), col(96), ALU.mult, ALU.add)
            fw.stt(t[:, N_], r32[:, N_], col(80), k2[:, N_], ALU.mult, ALU.mult)
            pr = self.psum()
            fw.mm(pr[:, N_], cst[:, 2432:2560], t[:, N_])
            fw.tt(t[:, N_], pr[:, N_], v32[:, N_], ALU.mult)
            fw.tt(y32[:, N_], y32[:, N_], t[:, N_], ALU.add)
            fw.tt(W["mix"][:, c, N_], y32[:, N_], g32[:, N_], ALU.mult)
        self.nrot = 6
        fw.dma(fw.sp, R["w_r"], self.sc_["w_o%d" % j].re("(c p) n -> p c n", p=128))

    def work_init(self):
        fw = self.fw
        sb = self.fw.sb
        W = self.W = {}
        W["qe"] = sb("qe", [128, 2, 128], BF16)
        W["ke"] = sb("ke", [128, 2, 128], BF16)
        W["qepad"] = fw.av(90368, [128, NSS, 64], BF16)
        W["kl"] = sb("kl", [128, 2, 128], BF16)
        W["kltm"] = sb("kltm", [128, 2, 128], BF16)
        W["klpad"] = fw.av(92416, [128, NSS, 128], BF16)
        W["k32"] = fw.av(101696, [128, 2, 128], F32)
        W["vtm"] = fw.av(105792, [128, 512], BF16)
        W["gl"] = sb("gl", [16, 128], BF16)
        W["la"] = fw.av(102720, [128, 2, 128], F32)
        W["cum"] = fw.av(103744, [128, 2, 128], F32)
        W["e1"] = fw.av(104768, [128, 2, 128], F32)
        W["dcy"] = sb("dcy", [128, 2, NSS])
        W["att"] = sb("att", [128, 128], BF16)
        W["osq"] = sb("osq", [128, 128], BF16)
        W["rstd"] = sb("rstd", [128, 128])
        W["sr"] = sb("sr", [128, 128])
        W["on"] = sb("on", [128, 128])
        W["mix"] = fw.av(110592, [128, 8, 128], BF16)
        W["Sb"] = self.A["Sb"]
        W["xsq"] = sb("xsq", [128, 8, 128])
        W["mean"] = sb("mean", [128, 128])
        W["hm"], W["hv"], W["ht"] = self.A["hm"], self.A["hv"], self.A["ht"]
        W["vv"] = sb("vv", [128, 16, 16])
        W["idx"] = sb("idx", [128, 16, 16], U32)
        W["idxf"] = sb("idxf", [128, 16, 16])
        W["sc"] = sb("sc", [128, 8, 16])
        W["pos"] = sb("pos", [128, 8, 16], U32)
        W["pa"] = sb("pa", [128, 8, 16], U32)
        W["paf"] = sb("paf", [128, 8, 16])
        W["pbf"] = sb("pbf", [128, 8, 16])
        W["pe1"] = sb("pe1", [128, 8, 16])
        W["pe2"] = sb("pe2", [128, 8, 16])
        W["pg"] = sb("pg", [128, 8, 16])
        W["pz"] = sb("pz", [128, 8])
        W["e1T"] = sb("e1T", [128, 128])
        W["e2T"] = sb("e2T", [128, 128])
        W["gT"] = sb("gT", [128, 128])
        W["pT"] = [sb("pT%d" % k, [128, 128], BF16) for k in range(8)]
        W["ysb"] = self.A["ysb"]
        for nm in ("Vtm", "bGtm", "kGtm", "Bt", "Kt", "bG", "kG", "vbf", "g1x"):
            W[nm] = sb("h_" + nm, [128, 128], BF16)
        W["hd"] = []
        for hp in range(2):
            Hd = {nm: sb("hd%d_%s" % (hp, nm), [128, 128], BF16) for nm in ("MT", "MrT", "NT", "NrT", "Pa", "Pb", "PTa", "PTb", "TT")}
            Hd["Wsb"] = sb("hd%d_Wsb" % hp, [128, 64], BF16)
            Hd["Zsb"] = sb("hd%d_Zsb" % hp, [128, 64], BF16)
            W["hd"].append(Hd)
        W["AR"] = sb("h_AR", [128, 2, 128], BF16)
        for nm in ("Az", "Bz", "Kz"):
            W[nm] = sb("h_" + nm, [128, 2, 128], BF16)
        W["w1x"] = sb("h_w1x", [64, 128], BF16)
        W["a1x"] = sb("h_a1x", [64, 128], BF16)
        W["v1x"] = sb("h_v1x", [32, 128], BF16)
        W["gam"] = sb("h_gam", [128, NSS])
        W["kez"] = fw.av(106816, [128, 2, 2, 128], BF16)
        W["xcat"] = fw.av(96512, [128, 4, 131], F32)
        W["xc"] = fw.av(98624, [128, 4, 128], F32)
        W["xcb"] = fw.av(100672, [128, 4, 128], BF16)
        W["ga"] = sb("ga", [128, 128])
        W["gx"] = sb("gx", [128, 128])
        W["at"] = sb("at", [128, 128])
        W["bt"] = sb("bt", [128, 128])
        W["hh"] = sb("hh", [128, 128])
        W["gg4"] = fw.av(107840, [128, 4, 128], F32)
        W["t1"] = sb("t1", [128, 128])

    def gla_lru_tile(self, j, kind, n):
        fw = self.fw
        W = self.W
        xb = self.xb
        w_in = self.w_in
        nseg, L = (1, n) if kind == "p" else (NSS, TS)
        evv = self.evv
        S = self.gla_S[j] if kind == "p" else self.gla_Ss[j]

        def proj(col0, ncols, ps):
            for c in range(8):
                fw.mm(ps[0:ncols, 0:n], w_in[:, c, col0:col0 + ncols], xb[:, c, 0:n], start=(c == 0), stop=(c == 7))

        import os
        STG = int(os.environ.get("STG", "99"))
        if STG < 1:
            return
        ps = self.psum()
        proj(1024, 16, ps)
        fw.cp(W["gl"][:, 0:n], ps[0:16, 0:n])
        for c in range(2):
            ps = self.psum()
            fw.mm(ps[:, 0:n], self.w_gate[:, c * 128:(c + 1) * 128], W["gl"][:, 0:n])
            fw.ts(W["t1"][:, 0:n], ps[:, 0:n], evv[:, c:c + 1], -1.0, ALU.add, ALU.mult)
            fw.actv(W["t1"][:, 0:n], W["t1"][:, 0:n], AF.Exp)
            fw.actv(W["la"][:, c, 0:n], W["t1"][:, 0:n], AF.Ln, bias=self.cst[:, 640:641])
        if STG < 2:
            return
        fw.ts(W["la"][:, :, 0:n], W["la"][:, :, 0:n], -1.0 / 16.0, None, ALU.mult)
        rmask = self.cst[:, 768:896] if kind == "s" else self.ones
        for c in range(2):
            fw.scan(W["cum"][:, c, 0:n], rmask[:, 0:n], W["la"][:, c, 0:n], 0.0)
        cumv = W["cum"][:, :, 0:n].re("p c (s l) -> p c s l", s=nseg)
        cumC = cumv[:, :, :, L - 1:L]
        if STG < 3:
            return
        for c in range(2):
            ps = self.psum()
            proj(c * 128, 128, ps)
            fw.actv(W["e1"][:, c, 0:n], W["cum"][:, c, 0:n], AF.Exp)
            fw.stt(W["qe"][:, c, 0:n], ps[:, 0:n], 0.125, W["e1"][:, c, 0:n], ALU.mult, ALU.mult)
            ps = self.psum()
            proj(256 + c * 128, 128, ps)
            fw.cp(W["k32"][:, c, 0:n], ps[:, 0:n])
            fw.actv(W["e1"][:, c, 0:n], W["cum"][:, c, 0:n], AF.Exp, scale=-1.0)
            fw.tt(W["ke"][:, c, 0:n], W["k32"][:, c, 0:n], W["e1"][:, c, 0:n], ALU.mult)
        if STG < 4:
            return
        e1v = W["e1"][:, :, 0:n].re("p c (s l) -> p c s l", s=nseg)
        fw.tt(e1v, cumC.bc([128, 2, nseg, L]), cumv, ALU.subtract)
        fw.actv(W["e1"][:, :, 0:n], W["e1"][:, :, 0:n], AF.Exp)
        fw.tt(W["kl"][:, :, 0:n], W["k32"][:, :, 0:n], W["e1"][:, :, 0:n], ALU.mult)
        fw.actv(W["dcy"][:, :, 0:nseg], cumC.re("p c s o -> p c (s o)"), AF.Exp)
        if STG < 5:
            return
        for c in range(2):
            ps = self.psum()
            psb = ps.bitcast(BF16)
            fw.tr(psb[0:n, 0:128], W["kl"][:, c, 0:n], self.identb)
            fw.cp(W["kltm"][0:n, c, :], psb[0:n, 0:128])
        if STG < 6:
            return
        ps = self.psum()
        for c in range(8):
            fw.mm(ps[0:n, 0:512], xb[:, c, 0:n], w_in[:, c, 512:1024], start=(c == 0), stop=(c == 7))
        fw.cp(W["vtm"][0:n, :], ps[0:n, :], eng=fw.act)
        Sb = W["Sb"]
        Sv = S if kind == "s" else S.re("p (s c) v -> p s c v", s=1)
        for h in range(4):
            fw.ts(Sb[:, 0:nseg, h, :], Sv[:, :, h // 2, :], self.cst[:, 900 + h % 2:901 + h % 2], None, ALU.mult)
        for c in range(2):
            for hp in range(2):
                fw.ts(W["kez"][:, c, hp, 0:n], W["ke"][:, c, 0:n], self.cst[:, 900 + hp:901 + hp], None, ALU.mult)
        if STG < 7:
            return
        mask = self.m_incl if kind == "p" else self.m_sincl
        for h in range(4):
            c, hp = h // 2, h % 2
            pr = slice(hp * 64, hp * 64 + 64)
            ps = self.psum()
            fw.mm(ps[0:n, 0:n], W["kez"][:, c, hp, 0:n], W["qe"][:, c, 0:n])
            fw.tt(W["att"][0:n, 0:n], ps[0:n, 0:n], mask[0:n, 0:n], ALU.mult)
            SUB = float(os.environ.get("SUB", "99"))
            if SUB < 1:
                continue
            po = self.psum()
            fw.mm(po[:, 0:n], W["vtm"][0:n, h * 128:(h + 1) * 128], W["att"][0:n, 0:n], start=True, stop=False)
            if kind == "p":
                fw.mm(po[:, 0:n], Sb[:, 0, h, :], W["qe"][:, c, 0:n], start=False, stop=True)
            else:
                if hp == 0:
                    fw.tt(W["qepad"], W["qe"][:, c:c + 1, 0:n].bc([128, NSS, n]),
                          self.cst[:, 1024:2048].re("p (s t) -> p s t", s=NSS), ALU.mult)
                for s in range(nseg):
                    fw.mm(po[:, 0:n], Sb[:, s, h, :], W["qepad"][:, s, :], start=False, stop=(s == nseg - 1))
            if SUB < 2:
                continue
            fw.actv(W["osq"][:, 0:n], po[:, 0:n], AF.Square)
            pm = self.psum()
            fw.mm(pm[:, 0:n], self.onesb, W["osq"][:, 0:n])
            fw.actv(W["rstd"][:, 0:n], pm[:, 0:n], AF.Ln, bias=self.cst[:, 896:897], scale=1.0 / 128.0)
            fw.actv(W["rstd"][:, 0:n], W["rstd"][:, 0:n], AF.Exp, scale=-0.5)
            fw.stt(W["on"][:, 0:n], po[:, 0:n], evv[:, 2:3], W["rstd"][:, 0:n], ALU.mult, ALU.mult)
            if SUB < 3:
                continue
            pr_ = self.psum()
            proj(1040 + h * 128, 128, pr_)
            self.sigm(W["sr"][:, 0:n], pr_[:, 0:n])
            fw.tt(W["sr"][:, 0:n], W["sr"][:, 0:n], pr_[:, 0:n], ALU.mult)
            fw.tt(W["mix"][:, h, 0:n], W["on"][:, 0:n], W["sr"][:, 0:n], ALU.mult)
        if STG < 8:
            return
        if kind == "p":
            for c in range(2):
                ps = self.psum()
                for hp in range(2):
                    h = c * 2 + hp
                    fw.mm(ps[hp * 64:hp * 64 + 64, 0:128], W["kltm"][0:n, c, hp * 64:hp * 64 + 64], W["vtm"][0:n, h * 128:(h + 1) * 128])
                fw.stt(S[:, c, :], S[:, c, :], W["dcy"][:, c, 0:1], ps[:, 0:128], ALU.mult, ALU.add)
        else:
            for c in range(2):
                fw.tt(W["klpad"][0:n], W["kltm"][0:n, c:c + 1, :].bc([n, NSS, 128]),
                      self.cst[0:n, 896 + 16:896 + 32].re("p (s o) -> p s o", o=1).bc([n, NSS, 128]), ALU.mult)
                for s0 in range(0, NSS, 4):
                    ps = self.psum()
                    for s in range(s0, s0 + 4):
                        for hp in range(2):
                            h = c * 2 + hp
                            fw.mm(ps[hp * 64:hp * 64 + 64, (s - s0) * 128:(s - s0 + 1) * 128], W["klpad"][0:n, s, hp * 64:hp * 64 + 64],
                                  W["vtm"][0:n, h * 128:(h + 1) * 128])
                    for s in range(s0, s0 + 4):
                        fw.stt(S[:, s, c, :], S[:, s, c, :], W["dcy"][:, c, s:s + 1], ps[:, (s - s0) * 128:(s - s0 + 1) * 128], ALU.mult, ALU.add)

        if STG < 9:
            return
        cv = self.lru_cv[j] if kind == "p" else self.lru_cvs[j]
        hst = self.lru_h[j] if kind == "p" else self.lru_hs[j]
        LL = L + 3
        xcat = W["xcat"][:, :, 0:nseg * LL].re("p c (s l) -> p c s l", s=nseg)
        fw.cp(xcat[:, :, :, 0:3], cv)
        for c in range(4):
            ps = self.psum()
            proj(1552 + c * 128, 128, ps)
            fw.cp(xcat[:, c, :, 3:LL], ps[:, 0:n].re("p (s l) -> p s l", s=nseg), eng=fw.act)
        fw.cp(cv, xcat[:, :, :, L:LL])
        for c in range(4):
            xcv = W["xc"][:, c, 0:n].re("p (s l) -> p s l", s=nseg)
            fw.ts(xcv, xcat[:, c, :, 0:L], evv[:, 7 + c:8 + c], evv[:, 3 + c:4 + c], ALU.mult, ALU.add)
            for i in range(1, 4):
                fw.stt(xcv, xcat[:, c, :, i:i + L], evv[:, 7 + 4 * i + c:8 + 4 * i + c], xcv, ALU.mult, ALU.add)
        if STG < 10:
            return
        fw.cp(W["xcb"][:, :, 0:n], W["xc"][:, :, 0:n])
        for c in range(4):
            ps = self.psum()
            proj(2064 + c * 128, 128, ps)
            fw.actv(W["gg4"][:, c, 0:n], ps[:, 0:n], AF.Gelu_apprx_tanh)
        for c in range(4):
            ps = self.psum()
            fw.mm(ps[:, 0:n], self.lruw[:, 0, c, :], W["xcb"][:, c, 0:n])
            self.sigm(W["ga"][:, 0:n], ps[:, 0:n], self.negevv[:, 23 + c:24 + c])
            ps = self.psum()
            fw.mm(ps[:, 0:n], self.lruw[:, 1, c, :], W["xcb"][:, c, 0:n])
            self.sigm(W["gx"][:, 0:n], ps[:, 0:n], self.negevv[:, 27 + c:28 + c])
            fw.actv(W["at"][:, 0:n], W["ga"][:, 0:n], AF.Exp, scale=self.c8[j][:, c:c + 1])
            fw.tt(W["t1"][:, 0:n], W["at"][:, 0:n], W["at"][:, 0:n], ALU.mult)
            fw.ts(W["t1"][:, 0:n], W["t1"][:, 0:n], -1.0, 1.0, ALU.mult, ALU.add)
            fw.actv(W["t1"][:, 0:n], W["t1"][:, 0:n], AF.Ln)
            fw.actv(W["t1"][:, 0:n], W["t1"][:, 0:n], AF.Exp, scale=0.5)
            fw.tt(W["bt"][:, 0:n], W["gx"][:, 0:n], W["xc"][:, c, 0:n], ALU.mult)
            fw.tt(W["bt"][:, 0:n], W["bt"][:, 0:n], W["t1"][:, 0:n], ALU.mult)
            atv = W["at"][:, 0:n].re("p (s l) -> p s l", s=nseg)
            btv = W["bt"][:, 0:n].re("p (s l) -> p s l", s=nseg)
            h0 = hst[:, c, :].re("p (s o) -> p s o", o=1)
            fw.tt(W["t1"][:, 0:nseg].re("p (s o) -> p s o", o=1), atv[:, :, 0:1], h0, ALU.mult)
            fw.tt(btv[:, :, 0:1], btv[:, :, 0:1], W["t1"][:, 0:nseg].re("p (s o) -> p s o", o=1), ALU.add)
            fw.memset(atv[:, :, 0:1], 0.0)
            fw.scan(W["hh"][:, 0:n], W["at"][:, 0:n], W["bt"][:, 0:n], 0.0)
            hv = W["hh"][:, 0:n].re("p (s l) -> p s l", s=nseg)
            fw.cp(h0, hv[:, :, L - 1:L])
            fw.tt(W["mix"][:, 4 + c, 0:n], W["hh"][:, 0:n], W["gg4"][:, c, 0:n], ALU.mult)


def host_consts():
    c = np.zeros((128, 2816), np.float32)
    c[:, 0:128] = np.eye(128)
    s = np.arange(128)[:, None]
    t = np.arange(128)[None, :]
    c[:, 128:256] = (s <= t)
    c[:, 256:384] = (s < t)
    same = (s // 4) == (t // 4)
    c[:, 384:512] = (s <= t) & same
    c[:, 512:640] = (s < t) & same
    c[:, 640:768] = 1.0
    c[:, 768:896] = (np.arange(128) % 4 != 0)[None, :]
    c[:, 896] = LN_EPS
    c[0:64, 900] = 1.0
    c[64:128, 901] = 1.0
    c[:, 912:928] = (np.arange(128)[:, None] // 4 == np.arange(16)[None, :])
    c[:, 2048:2176] = np.arange(128)[None, :]
    c[:, 2176:2304] = (t < s)
    c[:, 2304:2432] = (t < s) & same
    blk = (s // 64) == (t // 64)
    c[:, 2432:2560] = blk
    c[:, 2560:2688] = blk / 64.0
    c[:, 897] = -0.5
    c[:, 898] = 64e-5
    c[:, 1024:2048] = (np.arange(64)[None, :] // 4 == np.arange(16)[:, None]).reshape(1, 1024)
    return c


def _fm(v, nch):
    return np.ascontiguousarray(np.asarray(v, np.float32).reshape(nch, 128).T)


def pack_even(inp):
    vec = np.zeros((2, 128, 40), np.float32)
    lruw = np.zeros((2, 2, 4, 128, 128), np.float32)
    for j in range(2):
        vec[j, :, 0:2] = _fm(inp["ev_gla_b_gate"][j], 2)
        vec[j, :, 2] = inp["ev_gla_norm"][j]
        vec[j, :, 3:7] = _fm(inp["ev_conv_b"][j], 4)
        for i in range(4):
            vec[j, :, 7 + 4 * i:11 + 4 * i] = _fm(inp["ev_conv_w"][j, i], 4)
        vec[j, :, 23:27] = _fm(inp["ev_lru_ba"][j], 4)
        vec[j, :, 27:31] = _fm(inp["ev_lru_bx"][j], 4)
        vec[j, :, 31:35] = _fm(inp["ev_lru_lambda"][j], 4)
        for g, nm in enumerate(("ev_lru_wa", "ev_lru_wx")):
            for c in range(4):
                for nb in range(2):
                    lruw[j, g, c, nb * 64:(nb + 1) * 64, nb * 64:(nb + 1) * 64] = inp[nm][j, 2 * c + nb]
    return vec, lruw


def make_in_maps(inp):
    inp = {k: np.asarray(v) for k, v in inp.items()}
    vec, lruw = pack_even(inp)
    consts = host_consts()
    lnp = np.zeros((128, 128), np.float32)
    lnp[:, 0:64] = inp["ln_g"].reshape(64, 128).T
    lnp[:, 64:128] = inp["ln_b"].reshape(64, 128).T
    keysT = np.ascontiguousarray(inp["peer_keys"].transpose(0, 4, 1, 2, 3).reshape(DEPTH, 128, 2048))
    uT = np.ascontiguousarray(inp["peer_u"].transpose(0, 2, 1))
    v1p = np.zeros((2, D, 32), np.float32)
    v1p[1] = inp["od_v1"][0]
    od_l1 = np.ascontiguousarray(np.concatenate([inp["od_w1"], inp["od_a1"], inp["od_g1"], v1p], axis=2))
    od_v2 = np.zeros((2, 32, D), np.float32)
    od_v2[1] = inp["od_v2"][0]
    od_vecs = np.zeros((2, 128, 112), np.float32)
    for j in range(2):
        for i in range(6):
            od_vecs[j, :, i * 8:(i + 1) * 8] = _fm(inp["od_mu"][j, i], 8)
        for k_, nm in enumerate(("od_w0", "od_a0", "od_k_k", "od_k_a", "od_r_k", "od_lnx_g", "od_lnx_b")):
            od_vecs[j, :, 48 + 8 * k_:56 + 8 * k_] = _fm(inp[nm][j].reshape(-1), 8)
    od_vecs[1, :, 104:112] = _fm(inp["od_v0"][0], 8)
    maps = []
    for c in range(NCORES):
        xp = np.concatenate([inp["meta_tokens"], inp["x_prompt"][c]], axis=0)
        xs = inp["x_sample"][NSS * c:NSS * (c + 1)].reshape(NSS * TS, D)
        sl = slice(NSS * c, NSS * (c + 1))
        g = inp["state_gla"][:, sl].reshape(2, NSS, 2, 2, 64, 128)
        g = g.transpose(0, 3, 4, 1, 2, 5).reshape(2, 128, NSS * 2 * 128)
        hh = inp["state_lru_h"][:, sl].reshape(2, NSS, 4, 128).transpose(0, 3, 2, 1).reshape(2, 128, 4 * NSS)
        cv = inp["state_lru_conv"][:, sl].reshape(2, NSS, 3, 4, 128).transpose(0, 4, 3, 1, 2).reshape(2, 128, 4 * NSS * 3)
        rw = inp["state_rwkv"][:, sl].reshape(2, NSS, 8, 2, 64, 64)
        rw = rw.transpose(0, 2, 3, 5, 1, 4).reshape(2, 8, 128, NSS * 64)
        shh = inp["state_rwkv_shift"][:, sl].reshape(2, NSS, 8, 128).transpose(0, 3, 2, 1).reshape(2, 128, 8 * NSS)
        m = dict(
            xT_p=np.ascontiguousarray(xp.T), xT_s=np.ascontiguousarray(xs.T), consts=consts,
            ev_w_in=inp["ev_w_in"], ev_w_gate=inp["ev_gla_w_gate"], ev_vecs=vec, ev_lruw=lruw,
            ev_w_out=inp["ev_w_out"], st_gla=np.ascontiguousarray(g), st_h=np.ascontiguousarray(hh),
            st_conv=np.ascontiguousarray(cv),
            od_w_r=inp["od_w_r"], od_w_k=inp["od_w_k"], od_w_v=inp["od_w_v"], od_w_o=inp["od_w_o"],
            od_l1=od_l1, od_w2=inp["od_w2"], od_a2=inp["od_a2"], od_v2=od_v2, od_g2=inp["od_g2"], od_vecs=od_vecs,
            st_rw=np.ascontiguousarray(rw), st_sh=np.ascontiguousarray(shh),
            lnp=lnp, peer_wq=inp["peer_w_q"], peer_keysT=keysT, peer_uT=uT, peer_v=inp["peer_v"],
        )
        maps.append(m)
    return maps


def assemble(results):
    out = {}
    p_gla = np.zeros((2, NCORES, 4, 64, 128), np.float32)
    s_gla = np.zeros((2, NCORES * NSS, 4, 64, 128), np.float32)
    p_h = np.zeros((2, NCORES, 512), np.float32)
    s_h = np.zeros((2, NCORES * NSS, 512), np.float32)
    p_cv = np.zeros((2, NCORES, 3, 512), np.float32)
    s_cv = np.zeros((2, NCORES * NSS, 3, 512), np.float32)
    for c, r in enumerate(results):
        sl = slice(NSS * c, NSS * (c + 1))
        g = r["o_gla_p"].reshape(2, 2, 64, 2, 128)
        p_gla[:, c] = g.transpose(0, 3, 1, 2, 4).reshape(2, 4, 64, 128)
        g = r["o_gla_s"].reshape(2, 2, 64, NSS, 2, 128)
        s_gla[:, sl] = g.transpose(0, 3, 4, 1, 2, 5).reshape(2, NSS, 4, 64, 128)
        p_h[:, c] = r["o_h_p"].reshape(2, 128, 4).transpose(0, 2, 1).reshape(2, 512)
        s_h[:, sl] = r["o_h_s"].reshape(2, 128, 4, NSS).transpose(0, 3, 2, 1).reshape(2, NSS, 512)
        p_cv[:, c] = r["o_conv_p"].reshape(2, 128, 4, 3).transpose(0, 3, 2, 1).reshape(2, 3, 512)
        s_cv[:, sl] = r["o_conv_s"].reshape(2, 128, 4, NSS, 3).transpose(0, 3, 4, 2, 1).reshape(2, NSS, 3, 512)
    p_rw = np.zeros((2, NCORES, 16, 64, 64), np.float32)
    s_rw = np.zeros((2, NCORES * NSS, 16, 64, 64), np.float32)
    p_sh = np.zeros((2, NCORES, D), np.float32)
    s_sh = np.zeros((2, NCORES * NSS, D), np.float32)
    for c, r in enumerate(results):
        sl = slice(NSS * c, NSS * (c + 1))
        a = r["o_rw_p"].reshape(2, 2, 64, 8, 64)
        p_rw[:, c] = a.transpose(0, 3, 1, 4, 2).reshape(2, 16, 64, 64)
        a = r["o_rw_s"].reshape(2, 8, 2, 64, NSS, 64)
        s_rw[:, sl] = a.transpose(0, 4, 1, 2, 5, 3).reshape(2, NSS, 16, 64, 64)
        p_sh[:, c] = r["o_sh_p"].reshape(2, 128, 8).transpose(0, 2, 1).reshape(2, D)
        s_sh[:, sl] = r["o_sh_s"].reshape(2, 128, 8, NSS).transpose(0, 3, 2, 1).reshape(2, NSS, D)
    out.update(p_rw=p_rw, s_rw=s_rw, p_sh=p_sh, s_sh=s_sh)
    out.update(p_gla=p_gla, s_gla=s_gla, p_h=p_h, s_h=s_h, p_cv=p_cv, s_cv=s_cv)
    return out


def kernel(**inputs):
    prog = Prog()
    nc = prog.build()
    maps = make_in_maps(inputs)
    res = run_bass_kernel_spmd(nc, maps, core_ids=list(range(NCORES)))
    o = assemble(res.results)
    z = lambda *s: np.zeros(s, np.float32)
    y_p = np.stack([np.ascontiguousarray(r["o_yT"][:, NMETA:TP].T) for r in res.results])
    y_s = np.concatenate([np.ascontiguousarray(r["o_yT"][:, TP:].T).reshape(NSS, TS, D) for r in res.results])
    return (y_p, y_s, o["p_gla"], o["p_h"], o["p_cv"], o["p_rw"], o["p_sh"],
            o["s_gla"], o["s_h"], o["s_cv"], o["s_rw"], o["s_sh"])
```
